# Optimizing a Trainium2 kernel written in Bass

```python
import jax
import jax.numpy as jnp
from jax import lax
import numpy as np

D_MODEL = 1024
BATCH = 8
SEQ = 4096
DEPTH = 1

GRID_W = 64
CTX_LEN = 256
EPS = 1e-6
MLA_HEADS = 8
MLA_NOPE = 64
MLA_ROPE = 32
MLA_V_HEAD = 64
MLA_Q_LORA = 384
MLA_KV_LORA = 256
MLA_WIDTH = MLA_HEADS * MLA_V_HEAD
MLA_SCALE = (MLA_NOPE + MLA_ROPE) ** -0.5
ROPE_AXIS_DIM = MLA_ROPE // 2
ROPE_THETA = 10000.0
Q_BLOCK = 128
GLA_HEADS = 4
GLA_HEAD_K = 128
GLA_HEAD_V = 128
GLA_KEY = GLA_HEADS * GLA_HEAD_K
GLA_VALUE = GLA_HEADS * GLA_HEAD_V
GLA_QSCALE = GLA_HEAD_K ** -0.5
GATE_RANK = 16
GATE_NORM = 16.0
GLA_CHUNK = 64
D_FF = 2816
CONV_W = 3
IN_SPLITS = (MLA_Q_LORA, MLA_KV_LORA, MLA_ROPE, GLA_KEY, GLA_KEY, GLA_VALUE, GLA_VALUE, 2 * GATE_RANK, 2 * D_MODEL)
IN_COLS = sum(IN_SPLITS)

kernel_name = "hybrid_mla_gla_convglu_dit"


def rms_norm(x, g):
    xf = x.astype(jnp.float32)
    y = xf * lax.rsqrt(jnp.mean(xf * xf, axis=-1, keepdims=True) + EPS)
    return (y * g.astype(jnp.float32)).astype(x.dtype)


def modulate(h, shift, scale):
    return h * (1 + scale) + shift


def split_columns(z):
    offsets = [int(o) for o in np.cumsum(IN_SPLITS)[:-1]]
    return jnp.split(z, offsets, axis=-1)


def heads(z, n_heads):
    return z.reshape(z.shape[:-1] + (n_heads, z.shape[-1] // n_heads))


def axial_angles(length):
    t = jnp.arange(length, dtype=jnp.int32)
    row = (t // GRID_W).astype(jnp.float32)
    col = (t % GRID_W).astype(jnp.float32)
    inv_freq = ROPE_THETA ** (-jnp.arange(0, ROPE_AXIS_DIM, 2, dtype=jnp.float32) / ROPE_AXIS_DIM)
    return row[:, None] * inv_freq, col[:, None] * inv_freq


def rotate_axis(x, ang):
    x1, x2 = jnp.split(x, 2, axis=-1)
    cos, sin = jnp.cos(ang), jnp.sin(ang)
    return jnp.concatenate([x1 * cos - x2 * sin, x2 * cos + x1 * sin], axis=-1)


def rope_2d(x, ang_row, ang_col):
    xf = x.astype(jnp.float32)
    xr, xc = xf[..., :ROPE_AXIS_DIM], xf[..., ROPE_AXIS_DIM:]
    return jnp.concatenate([rotate_axis(xr, ang_row), rotate_axis(xc, ang_col)], axis=-1).astype(x.dtype)


def mla_queries(q_c, q_norm_g, w_uq):
    q = heads(rms_norm(q_c, q_norm_g) @ w_uq, MLA_HEADS)
    return q[..., :MLA_NOPE], q[..., MLA_NOPE:]


def mla_keys_values(kv_c, kv_norm_g, w_ukv):
    kv = heads(rms_norm(kv_c, kv_norm_g) @ w_ukv, MLA_HEADS)
    return kv[..., :MLA_NOPE], kv[..., MLA_NOPE:]


def attention(q_nope, q_rope, k_nope, k_rope, v):
    b, lq = q_nope.shape[:2]
    nb = lq // Q_BLOCK

    def blocks(z):
        return jnp.moveaxis(z.reshape((b, nb, Q_BLOCK) + z.shape[2:]), 1, 0)

    def one_block(qs):
        qn, qr = qs
        s = jnp.einsum('bqhd,bkhd->bhqk', qn, k_nope) + jnp.einsum('bqhr,bkr->bhqk', qr, k_rope)
        p = jax.nn.softmax(s.astype(jnp.float32) * MLA_SCALE, axis=-1).astype(v.dtype)
        return jnp.einsum('bhqk,bkhd->bqhd', p, v)

    o = lax.map(one_block, (blocks(q_nope), blocks(q_rope)))
    return jnp.moveaxis(o, 0, 1).reshape(b, lq, MLA_WIDTH)


def to_chunks(z):
    b, l, h, d = z.shape
    return z.reshape(b, l // GLA_CHUNK, GLA_CHUNK, h, d).transpose(0, 3, 1, 2, 4)


def from_chunks(z):
    b, h, n, c, d = z.shape
    return z.transpose(0, 2, 3, 1, 4).reshape(b, n * c, h, d)


def chunk_state_terms(kc, vc, b_cum):
    b_last = b_cum[..., -1:, :]
    u = jnp.einsum('bhncd,bhncv->bhndv', kc * jnp.exp(b_last - b_cum), vc)
    decay = jnp.exp(b_last[..., 0, :])
    return decay, u


def scan_states(decay, u, s0):
    def step(s, inp):
        d, du = inp
        return d[..., None] * s + du, s

    s_final, s_in = lax.scan(step, s0, (jnp.moveaxis(decay, 2, 0), jnp.moveaxis(u, 2, 0)))
    return s_final, jnp.moveaxis(s_in, 0, 2)


def gla_direction(q, k, v, g, s0):
    qc, kc, vc = (to_chunks(z.astype(jnp.float32)) for z in (q, k, v))
    b_cum = jnp.cumsum(to_chunks(g), axis=3)
    decay, u = chunk_state_terms(kc, vc, b_cum)
    s_final, s_in = scan_states(decay, u, s0)
    qe = qc * jnp.exp(b_cum)
    mask = jnp.tril(jnp.ones((GLA_CHUNK, GLA_CHUNK), jnp.float32))
    a = jnp.einsum('bhncd,bhnsd->bhncs', qe, kc * jnp.exp(-b_cum)) * mask
    o = jnp.einsum('bhncs,bhnsv->bhncv', a, vc) + jnp.einsum('bhncd,bhndv->bhncv', qe, s_in)
    return from_chunks(o), s_final


def gla_final_state(k, v, g, s0):
    kc, vc = to_chunks(k.astype(jnp.float32)), to_chunks(v.astype(jnp.float32))
    decay, u = chunk_state_terms(kc, vc, jnp.cumsum(to_chunks(g), axis=3))
    s_final, _ = scan_states(decay, u, s0)
    return s_final


def flip(z):
    return jnp.flip(z, axis=1)


def gla_bidirectional(q, k, v, g_f, g_b, s_f, s_b):
    o_f, s_f_out = gla_direction(q, k, v, g_f, s_f)
    o_b, s_b_out = gla_direction(flip(q), flip(k), flip(v), flip(g_b), s_b)
    return o_f + flip(o_b), s_f_out, s_b_out


def gla_inputs(gq, gk, gv, glow, w_decay, b_decay):
    lf, lb = glow[..., :GATE_RANK], glow[..., GATE_RANK:]
    g_f = jax.nn.log_sigmoid((lf @ w_decay[0] + b_decay[0]).astype(jnp.float32)) / GATE_NORM
    g_b = jax.nn.log_sigmoid((lb @ w_decay[1] + b_decay[1]).astype(jnp.float32)) / GATE_NORM
    return (heads(gq, GLA_HEADS) * GLA_QSCALE, heads(gk, GLA_HEADS), heads(gv, GLA_HEADS),
            heads(g_f, GLA_HEADS), heads(g_b, GLA_HEADS))


def gla_branch(o, r, norm_g, w_br):
    y = rms_norm(o, norm_g).reshape(o.shape[0], o.shape[1], GLA_VALUE).astype(r.dtype) * jax.nn.silu(r)
    return y @ w_br


def merge_branches(br_mla, br_gla, gate_logits, w_out):
    g = jax.nn.sigmoid(gate_logits.astype(jnp.float32)).astype(br_mla.dtype)
    g_mla, g_gla = jnp.split(g, 2, axis=-1)
    return (g_mla * br_mla + g_gla * br_gla) @ w_out


def token_mixer(h, hc, w_in, q_norm_g, w_uq, kv_norm_g, w_ukv, w_decay, b_decay, gla_norm_g,
                w_br_mla, w_br_gla, w_out, ang_row, ang_col, update_ctx):
    q_c, kv_c, k_rope, gq, gk, gv, gr, glow, gate_logits = split_columns(h @ w_in)
    q_cc, kv_cc, k_rope_c, gq_c, gk_c, gv_c, gr_c, glow_c, gate_logits_c = split_columns(hc @ w_in)

    k_nope_c, v_c = mla_keys_values(kv_cc, kv_norm_g, w_ukv)
    k_nope, v = mla_keys_values(kv_c, kv_norm_g, w_ukv)
    q_nope, q_rope = mla_queries(q_c, q_norm_g, w_uq)
    q_rope = rope_2d(q_rope, ang_row[:, None, :], ang_col[:, None, :])
    k_rope = rope_2d(k_rope, ang_row, ang_col)
    o_mla = attention(q_nope, q_rope,
                      jnp.concatenate([k_nope, k_nope_c], axis=1),
                      jnp.concatenate([k_rope, k_rope_c], axis=1),
                      jnp.concatenate([v, v_c], axis=1))

    b = h.shape[0]
    s0 = jnp.zeros((b, GLA_HEADS, GLA_HEAD_K, GLA_HEAD_V), jnp.float32)
    qg_c, kg_c, vg_c, gf_c, gb_c = gla_inputs(gq_c, gk_c, gv_c, glow_c, w_decay, b_decay)
    qg, kg, vg, gf, gb = gla_inputs(gq, gk, gv, glow, w_decay, b_decay)
    if update_ctx:
        o_gla_c, s_f, s_b = gla_bidirectional(qg_c, kg_c, vg_c, gf_c, gb_c, s0, s0)
    else:
        s_f = gla_final_state(kg_c, vg_c, gf_c, s0)
        s_b = gla_final_state(flip(kg_c), flip(vg_c), flip(gb_c), s0)
    o_gla, _, _ = gla_bidirectional(qg, kg, vg, gf, gb, s_f, s_b)

    out_lat = merge_branches(o_mla @ w_br_mla, gla_branch(o_gla, gr, gla_norm_g, w_br_gla), gate_logits, w_out)
    out_ctx = None
    if update_ctx:
        q_nope_c, q_rope_c = mla_queries(q_cc, q_norm_g, w_uq)
        o_mla_c = attention(q_nope_c, q_rope_c, k_nope_c, k_rope_c, v_c)
        out_ctx = merge_branches(o_mla_c @ w_br_mla, gla_branch(o_gla_c, gr_c, gla_norm_g, w_br_gla),
                                 gate_logits_c, w_out)
    return out_lat, out_ctx


def depthwise_conv_grid(u, w, bias, rows, cols):
    b, l, ch = u.shape
    grid = u.reshape(b, rows, cols, ch)
    y = lax.conv_general_dilated(grid, w[:, :, None, :].astype(u.dtype), (1, 1), 'SAME',
                                 dimension_numbers=('NHWC', 'HWIO', 'NHWC'), feature_group_count=ch)
    return y.reshape(b, l, ch) + bias


def conv_ffn(h, w_up, conv_w, conv_b, w_down, rows, cols):
    val, gate = jnp.split(h @ w_up, 2, axis=-1)
    gate = depthwise_conv_grid(gate, conv_w, conv_b, rows, cols)
    return (jax.nn.gelu(gate, approximate=False) * val) @ w_down


def setup_inputs(seed: int = 0) -> dict:
    key = jax.random.key(seed)
    ks = jax.random.split(key, 24)
    f32 = jnp.float32

    def normal(k, shape, s=1.0):
        return s * jax.random.normal(k, shape, f32)

    def dense(k, shape, fan_in, gain=1.0):
        return normal(k, shape, gain * fan_in ** -0.5)

    def norm_gain(k, shape):
        return 1.0 + normal(k, shape, 0.05)

    return {
        "x": normal(ks[0], (BATCH, SEQ, D_MODEL)),
        "c": normal(ks[1], (BATCH, D_MODEL)),
        "ctx": normal(ks[2], (BATCH, CTX_LEN, D_MODEL)),
        "c_ctx": normal(ks[3], (D_MODEL,)),
        "w_ada": dense(ks[4], (DEPTH, D_MODEL, 6 * D_MODEL), D_MODEL, 0.5),
        "b_ada": normal(ks[5], (DEPTH, 6 * D_MODEL), 0.01),
        "norm1_g": norm_gain(ks[6], (DEPTH, D_MODEL)),
        "w_in": dense(ks[7], (DEPTH, D_MODEL, IN_COLS), D_MODEL),
        "q_norm_g": norm_gain(ks[8], (DEPTH, MLA_Q_LORA)),
        "w_uq": dense(ks[9], (DEPTH, MLA_Q_LORA, MLA_HEADS * (MLA_NOPE + MLA_ROPE)), MLA_Q_LORA),
        "kv_norm_g": norm_gain(ks[10], (DEPTH, MLA_KV_LORA)),
        "w_ukv": dense(ks[11], (DEPTH, MLA_KV_LORA, MLA_HEADS * (MLA_NOPE + MLA_V_HEAD)), MLA_KV_LORA),
        "gla_w_decay": dense(ks[12], (DEPTH, 2, GATE_RANK, GLA_KEY), GATE_RANK),
        "gla_b_decay": normal(ks[13], (DEPTH, 2, GLA_KEY), 0.1),
        "gla_norm_g": norm_gain(ks[14], (DEPTH, GLA_HEAD_V)),
        "w_br_mla": dense(ks[15], (DEPTH, MLA_WIDTH, D_MODEL), MLA_WIDTH),
        "w_br_gla": dense(ks[16], (DEPTH, GLA_VALUE, D_MODEL), GLA_VALUE),
        "w_out": dense(ks[17], (DEPTH, D_MODEL, D_MODEL), D_MODEL),
        "norm2_g": norm_gain(ks[18], (DEPTH, D_MODEL)),
        "w_up": dense(ks[19], (DEPTH, D_MODEL, 2 * D_FF), D_MODEL),
        "conv_w": dense(ks[20], (DEPTH, CONV_W, CONV_W, D_FF), CONV_W * CONV_W),
        "conv_b": normal(ks[21], (DEPTH, D_FF), 0.01),
        "w_down": dense(ks[22], (DEPTH, D_FF, D_MODEL), D_FF),
        "final_g": norm_gain(ks[23], (D_MODEL,)),
    }


def reference(x, c, ctx, c_ctx, w_ada, b_ada, norm1_g, w_in, q_norm_g, w_uq, kv_norm_g, w_ukv,
              gla_w_decay, gla_b_decay, gla_norm_g, w_br_mla, w_br_gla, w_out, norm2_g, w_up,
              conv_w, conv_b, w_down, final_g):
    length = x.shape[1]
    rows = length // GRID_W
    ang_row, ang_col = axial_angles(length)
    x_lat, x_ctx = x, ctx
    for layer in range(DEPTH):
        update_ctx = layer + 1 < DEPTH
        mod = jax.nn.silu(c) @ w_ada[layer] + b_ada[layer]
        mod_c = jax.nn.silu(c_ctx) @ w_ada[layer] + b_ada[layer]
        sh1, sc1, gt1, sh2, sc2, gt2 = jnp.split(mod[:, None, :], 6, axis=-1)
        sh1c, sc1c, gt1c, sh2c, sc2c, gt2c = jnp.split(mod_c, 6, axis=-1)

        h = modulate(rms_norm(x_lat, norm1_g[layer]), sh1, sc1)
        hc = modulate(rms_norm(x_ctx, norm1_g[layer]), sh1c, sc1c)
        mix, mix_c = token_mixer(h, hc, w_in[layer], q_norm_g[layer], w_uq[layer], kv_norm_g[layer],
                                 w_ukv[layer], gla_w_decay[layer], gla_b_decay[layer], gla_norm_g[layer],
                                 w_br_mla[layer], w_br_gla[layer], w_out[layer], ang_row, ang_col, update_ctx)
        x_lat = x_lat + gt1 * mix
        h = modulate(rms_norm(x_lat, norm2_g[layer]), sh2, sc2)
        x_lat = x_lat + gt2 * conv_ffn(h, w_up[layer], conv_w[layer], conv_b[layer], w_down[layer], rows, GRID_W)
        if update_ctx:
            x_ctx = x_ctx + gt1c * mix_c
            hc = modulate(rms_norm(x_ctx, norm2_g[layer]), sh2c, sc2c)
            x_ctx = x_ctx + gt2c * conv_ffn(hc, w_up[layer], conv_w[layer], conv_b[layer], w_down[layer],
                                            1, x_ctx.shape[1])
    return rms_norm(x_lat, final_g)
```

```python
import numpy as np
import ml_dtypes
from contextlib import ExitStack
import concourse.bass as bass
import concourse.mybir as mybir
from concourse.bass_utils import run_bass_kernel_spmd

F32 = mybir.dt.float32
BF16 = mybir.dt.bfloat16
AF = mybir.ActivationFunctionType
ALU = mybir.AluOpType

D = 1024
L = 4096
LC = 256
T = L + LC
NB = 9
EPS = 1e-6
MLA_SCALE = 96 ** -0.5
GLA_QSCALE = 128 ** -0.5
DFF = 2816
NJ = 22
GEN_MAX = 10000

O_QC, O_KVC, O_KR, O_GQ, O_GK, O_GV, O_GR, O_GLOW, O_GATE = 0, 384, 640, 672, 1184, 1696, 2208, 2720, 2752

V_BADA, V_G1, V_G2, V_FG, V_QG, V_KVG, V_GLAG, V_CB, V_CW, V_C, V_CC = 0, 48, 56, 64, 72, 75, 77, 78, 100, 298, 306
NV = 314


def blk_range(b):
    if b < 8:
        return b * 512, 512
    return L, LC


class Buf:
    __slots__ = ("name", "w", "r", "psum")

    def __init__(self, name):
        self.name = name
        self.w = None
        self.r = {}
        self.psum = False


class DSem:
    def __init__(self, sem):
        self.sem = sem
        self.cnt = 0


class Eng:
    def __init__(self, key, sems):
        self.key = key
        self.sems = sems
        self.gen = 0
        self.cnt = 0
        self.waited = {}
        self.prog = []

    @property
    def sem(self):
        return self.sems[self.gen]


class TT:
    def __init__(self, t, name):
        self.t = t
        self.b = Buf(name)


class StopBuild(Exception):
    pass


def build(debug=False, nphase=99):
    nc = bass.Bass("TRN2", target_bir_lowering=False)
    dkind = "ExternalOutput" if debug else "Internal"

    def din(name, shape, dt=F32):
        return nc.dram_tensor(name, list(shape), dt, kind="ExternalInput").ap()

    def dscr(name, shape, dt):
        return nc.dram_tensor(name, list(shape), dt, kind=dkind).ap()

    xT = din("xT", [D, T]).rearrange("(k p) t -> p k t", p=128)
    vecs_d = din("vecs", [128, NV])
    w_ada_d = din("w_ada", [D, 6 * D]).rearrange("(k p) n -> p k n", p=128)
    w_in_d = din("w_in", [D, 4800]).rearrange("(k p) n -> p k n", p=128)
    wkr2_d = din("wkr2", [D, 192]).rearrange("(k p) n -> p k n", p=128)
    w_uq_d = din("w_uq", [384, 768]).rearrange("(k p) n -> p k n", p=128)
    w_uqs_d = din("w_uqs", [384, 768]).rearrange("(k p) n -> p k n", p=128)
    w_ukv_d = din("w_ukv", [256, 1024]).rearrange("(k p) n -> p k n", p=128)
    wdec_d = din("wdec", [17, 1024])
    w_brm_d = din("w_brm", [512, D]).rearrange("(k p) n -> p k n", p=128)
    w_brg_d = din("w_brg", [512, D]).rearrange("(k p) n -> p k n", p=128)
    w_out_d = din("w_out", [D, D]).rearrange("(k p) n -> p k n", p=128)
    w_up_d = din("w_up", [D, 2 * DFF]).rearrange("(k p) n -> p k n", p=128)
    w_dn_d = din("w_dn", [DFF, D]).rearrange("(k p) n -> p k n", p=128)
    tri_d = din("tri", [128, 5, 128])
    cs_d = din("cs", [96, 2, L])
    outT = nc.dram_tensor("outT", [D, L], F32, kind="ExternalOutput").ap().rearrange("(k p) t -> p k t", p=128)

    win_s = dscr("win_s", [128, 8, 4800], BF16)
    wkr_s = dscr("wkr_s", [128, 8, 192], BF16)
    wuq_s = dscr("wuq_s", [128, 3, 768], BF16)
    wuqs_s = dscr("wuqs_s", [128, 3, 768], BF16)
    wukv_s = dscr("wukv_s", [128, 2, 1024], BF16)
    wdec_s = dscr("wdec_s", [17, 1024], BF16)
    wbrm_s = dscr("wbrm_s", [128, 4, D], BF16)
    wbrg_s = dscr("wbrg_s", [128, 4, D], BF16)
    wout_s = dscr("wout_s", [128, 8, D], BF16)
    wup_s = dscr("wup_s", [128, 8, 2 * DFF], BF16)
    wdn_s = dscr("wdn_s", [128, NJ, D], BF16)
    hT_s = dscr("hT_s", [128, 8, T], BF16)
    om_s = dscr("om_s", [128, 4, L], BF16)
    yT_s = dscr("yT_s", [128, 4, L], BF16)
    x1_s = dscr("x1_s", [128, 8, L], F32)

    st = ExitStack()
    with st:
        def sem(name):
            return st.enter_context(nc.semaphore(name))

        PE = Eng("pe", [sem(f"pe{i}") for i in range(4)])
        ACT = Eng("act", [sem(f"act{i}") for i in range(3)])
        DVE = Eng("dve", [sem(f"dve{i}") for i in range(3)])
        POOL = Eng("pool", [sem(f"pool{i}") for i in range(2)])
        SP = Eng("sp", [])
        ENGS = [PE, ACT, DVE, POOL, SP]
        dsems = [DSem(sem(f"d{i}")) for i in range(34)]
        sw_dsems = [DSem(sem(f"sw{i}")) for i in range(8)]
        ds_idx = [0]
        sw_idx = [0]
        allbufs = []

        phase_no = [0]
        stopped = [False]

        def newds(sw=False):
            if sw:
                d = sw_dsems[sw_idx[0]]
                sw_idx[0] += 1
                return d
            d = dsems[ds_idx[0]]
            ds_idx[0] += 1
            return d

        def issue(E, fn, R=(), W=(), ds=None):
            if stopped[0]:
                return
            need = {}

            def add(tok):
                if tok is None:
                    return
                s, val, key, ek, gen = tok
                if ek == E.key and ds is None:
                    if ek == "pe":
                        return
                    if gen == E.gen:
                        between = E.cnt - val
                    elif gen == E.gen - 1:
                        between = E.cnt + (GEN_MAX - val)
                    else:
                        between = 99
                    if between >= 3:
                        return
                if E.waited.get(key, 0) >= val:
                    return
                if key not in need or need[key][1] < val:
                    need[key] = (s, val)

            for b in R:
                add(b.w)
                if b.psum:
                    for tk in b.r.values():
                        if tk[3] != E.key:
                            add(tk)
            for b in W:
                add(b.w)
                for tk in b.r.values():
                    add(tk)
            waits = list(need.items())
            for key, (s, val) in waits:
                E.waited[key] = val
            if ds is None:
                if E.cnt >= GEN_MAX:
                    E.gen += 1
                    E.cnt = 0
                E.cnt += 1
                tok = (E.sem, E.cnt, E.sem.num, E.key, E.gen)
                isem, ival = E.sem, 1
            else:
                ds.cnt += 16
                tok = (ds.sem, ds.cnt, ds.sem.num, None, 0)
                isem, ival = ds.sem, 16

            def emit(e, waits=waits, fn=fn, isem=isem, ival=ival):
                for key, (s, val) in waits:
                    e.wait_ge(s, val)
                fn(e).then_inc(isem, ival)

            E.prog.append(emit)
            for b in W:
                b.w = tok
                b.r = {}
            for b in R:
                if b.w is tok:
                    continue
                old = b.r.get(tok[2])
                if old is None or old[1] < tok[1]:
                    b.r[tok[2]] = tok

        def flush_phase():
            if stopped[0]:
                return
            used = [d for d in dsems + sw_dsems if d.cnt > 0]

            def drain(e, used=[(d.sem, d.cnt) for d in used]):
                for s, v in used:
                    e.wait_ge(s, v)

            SP.prog.append(drain)
            with nc.Block() as blk:
                @blk.tensor
                def _(e):
                    for f in PE.prog:
                        f(e)

                @blk.scalar
                def _(e):
                    for f in ACT.prog:
                        f(e)

                @blk.vector
                def _(e):
                    for f in DVE.prog:
                        f(e)

                @blk.gpsimd
                def _(e):
                    for f in POOL.prog:
                        f(e)

                @blk.sync
                def _(e):
                    for f in SP.prog:
                        f(e)
            for E in ENGS:
                E.prog = []
            for b in allbufs:
                b.w = None
                b.r = {}
            ds_idx[0] = 0
            sw_idx[0] = 0
            phase_no[0] += 1
            if phase_no[0] >= nphase:
                stopped[0] = True

        def bl(xs):
            return [x.b if isinstance(x, TT) else x for x in xs]

        def mm(out, lhsT, rhs, start, stop, R, W):
            issue(PE, lambda e: e.matmul(out, lhsT=lhsT, rhs=rhs, start=start, stop=stop), bl(R), bl(W))

        def act(out, in_, func, R, W, bias=None, scale=None):
            kw = {}
            if bias is not None:
                kw["bias"] = bias
            if scale is not None:
                kw["scale"] = scale
            issue(ACT, lambda e: e.activation(out=out, in_=in_, func=func, **kw), bl(R), bl(W))

        def tt(E, out, in0, in1, op, R, W):
            issue(E, lambda e: e.tensor_tensor(out=out, in0=in0, in1=in1, op=op), bl(R), bl(W))

        def stt(E, out, in0, scalar, in1, op0, op1, R, W):
            issue(E, lambda e: e.scalar_tensor_tensor(out=out, in0=in0, scalar=scalar, in1=in1, op0=op0, op1=op1),
                  bl(R), bl(W))

        def ts(E, out, in0, s1, op0, R, W, s2=None, op1=None):
            if op1 is None:
                issue(E, lambda e: e.tensor_scalar(out=out, in0=in0, scalar1=s1, scalar2=None, op0=op0), bl(R), bl(W))
            else:
                issue(E, lambda e: e.tensor_scalar(out=out, in0=in0, scalar1=s1, scalar2=s2, op0=op0, op1=op1),
                      bl(R), bl(W))

        def cp(E, out, in_, R, W):
            if E is ACT:
                issue(E, lambda e: e.activation(out=out, in_=in_, func=AF.Copy), bl(R), bl(W))
            else:
                issue(E, lambda e: e.tensor_copy(out=out, in_=in_), bl(R), bl(W))

        def recip(out, in_, R, W):
            issue(DVE, lambda e: e.reciprocal(out=out, in_=in_), bl(R), bl(W))

        def mset(E, ap, val, W):
            issue(E, lambda e: e.memset(ap, val), [], bl(W))

        def dma(Q, out, in_, R, W, ds):
            issue(Q, lambda e: e.dma_start(out=out, in_=in_), bl(R), bl(W), ds=ds)

        uniq = [0]

        def sb(stack, name, shape, dt):
            uniq[0] += 1
            name = f"{name}_{uniq[0]}"
            t = stack.enter_context(nc.sbuf_tensor(name, list(shape), dt))
            x = TT(t, name)
            allbufs.append(x.b)
            return x

        PS = []
        PP = []
        for i in range(4):
            t = st.enter_context(nc.psum_tensor(f"psp{i}", [128, 1024], F32))
            PP.append(t)
            for hlf in range(2):
                x = TT(t[:, hlf * 512:(hlf + 1) * 512], f"psb{2 * i + hlf}")
                x.b.psum = True
                allbufs.append(x.b)
                PS.append(x)
        ps_rr = [0]

        class Pool_:
            def __init__(self, idxs):
                self.idxs = idxs
                self.i = 0

            def next(self):
                p = PS[self.idxs[self.i % len(self.idxs)]]
                self.i += 1
                return p

        vecs = sb(st, "vecs", [128, NV], F32)
        cst = sb(st, "cst", [128, 80], F32)
        ones_bf = sb(st, "ones_bf", [128, 128], BF16)
        ones_f = sb(st, "ones_f", [128, 128], F32)
        C_A1, C_B1, C_A1C, C_B1C, C_GT1, C_A2, C_B2, C_GT2, C_EPS, C_ONE = 0, 8, 16, 24, 32, 40, 48, 56, 64, 65
        eps_ap = cst.t[:, C_EPS:C_EPS + 1]

        late_pieces = []

        def build_phases():
            with ExitStack() as ph:
                vds = newds()
                dma(SP, vecs.t[:, :], vecs_d[:, :], [], [vecs], vds)
                mset(DVE, ones_bf.t[:, :], 1.0, [ones_bf])
                mset(DVE, ones_f.t[:, :], 1.0, [ones_f])
                mset(DVE, cst.t[:, C_EPS:C_EPS + 1], EPS, [cst])
                mset(DVE, cst.t[:, C_ONE:C_ONE + 1], 1.0, [cst])
                sT = sb(ph, "sT", [128, 16], F32)
                sT2 = sb(ph, "sT2", [128, 8, 2], F32)
                act(sT.t[:, :], vecs.t[:, V_C:V_C + 16], AF.Silu, [vecs], [sT])
                cp(DVE, sT2.t[:, :, 0], sT.t[:, 0:8], [sT], [sT2])
                cp(DVE, sT2.t[:, :, 1], sT.t[:, 8:16], [sT], [sT2])
                wst = [sb(ph, f"wada{i}", [128, 8, 512], F32) for i in range(2)]
                wds = [newds() for _ in range(2)]
                modp = PS[0]
                for pc in range(12):
                    w = wst[pc % 2]
                    dma(SP, w.t[:, :, :], w_ada_d[:, :, pc * 512:(pc + 1) * 512], [], [w], wds[pc % 2])
                    for cc in range(4):
                        col = pc * 4 + cc
                        for k in range(8):
                            mm(modp.t[:, col * 2:col * 2 + 2], w.t[:, k, cc * 128:(cc + 1) * 128], sT2.t[:, k, :],
                               k == 0, k == 7, [w, sT2], [modp])
                mod = sb(ph, "mod", [128, 48, 2], F32)
                cp(ACT, mod.t[:, :, :], modp.t[:, 0:96].rearrange("p (a b) -> p a b", b=2), [modp], [mod])
                modl = sb(ph, "modl", [128, 48], F32)
                modc = sb(ph, "modc", [128, 16], F32)
                tt(DVE, modl.t[:, :], mod.t[:, :, 0], vecs.t[:, V_BADA:V_BADA + 48], ALU.add, [mod, vecs], [modl])
                tt(DVE, modc.t[:, :], mod.t[:, 0:16, 1], vecs.t[:, V_BADA:V_BADA + 16], ALU.add, [mod, vecs], [modc])
                stt(DVE, cst.t[:, C_A1:C_A1 + 8], modl.t[:, 8:16], 1.0, vecs.t[:, V_G1:V_G1 + 8], ALU.add, ALU.mult,
                    [modl, vecs], [cst])
                cp(DVE, cst.t[:, C_B1:C_B1 + 8], modl.t[:, 0:8], [modl], [cst])
                stt(DVE, cst.t[:, C_A1C:C_A1C + 8], modc.t[:, 8:16], 1.0, vecs.t[:, V_G1:V_G1 + 8], ALU.add, ALU.mult,
                    [modc, vecs], [cst])
                cp(DVE, cst.t[:, C_B1C:C_B1C + 8], modc.t[:, 0:8], [modc], [cst])
                cp(DVE, cst.t[:, C_GT1:C_GT1 + 8], modl.t[:, 16:24], [modl], [cst])
                stt(DVE, cst.t[:, C_A2:C_A2 + 8], modl.t[:, 32:40], 1.0, vecs.t[:, V_G2:V_G2 + 8], ALU.add, ALU.mult,
                    [modl, vecs], [cst])
                cp(DVE, cst.t[:, C_B2:C_B2 + 8], modl.t[:, 24:32], [modl], [cst])
                cp(DVE, cst.t[:, C_GT2:C_GT2 + 8], modl.t[:, 40:48], [modl], [cst])

                stg = [sb(ph, f"stg{i}", [128, 4096], F32) for i in range(3)]
                stb = [sb(ph, f"stb{i}", [128, 4096], BF16) for i in range(3)]
                sds = [newds() for _ in range(3)]
                bds = [newds(True) for _ in range(3)]
                pieces = []

                def add_w(src, dst, nch, ncols, step, gcol=None, np_=128):
                    for c0 in range(0, ncols, step):
                        n = min(step, ncols - c0)
                        pieces.append((src[:, :, c0:c0 + n], dst[:, :, c0:c0 + n], nch, n, gcol, np_))

                add_w(w_in_d, win_s, 8, 4800, 480)
                add_w(wkr2_d, wkr_s, 8, 192, 192)
                add_w(w_uq_d, wuq_s, 3, 768, 768, V_QG)
                add_w(w_uqs_d, wuqs_s, 3, 768, 768, V_QG)
                add_w(w_ukv_d, wukv_s, 2, 1024, 1024, V_KVG)
                add_w(w_brm_d, wbrm_s, 4, D, D)
                add_w(w_brg_d, wbrg_s, 4, D, D)
                add_w(w_out_d, wout_s, 8, D, 512)
                add_w(w_up_d, wup_s, 8, 2 * DFF, 512)
                for j0 in range(0, NJ, 4):
                    nj = min(4, NJ - j0)
                    pieces.append((w_dn_d[:, j0:j0 + nj, :], wdn_s[:, j0:j0 + nj, :], nj, D, None, 128))
                cengs = [ACT, DVE, POOL]
                early = pieces[0:2] + pieces[10:15]
                late_pieces.extend(pieces[2:10] + pieces[15:])
                late_pieces.append((wdec_d[:, :], wdec_s[:, :], 1, 1024, None, 17))
                for i, (src, dst, nch, n, gcol, np_) in enumerate(early):
                    s_ = i % 3
                    sg, sbf = stg[s_], stb[s_]
                    sgv = sg.t[:, 0:nch * n].rearrange("p (c n) -> p c n", n=n)
                    sbv = sbf.t[:, 0:nch * n].rearrange("p (c n) -> p c n", n=n)
                    dma(SP, sgv, src, [], [sg], sds[s_])
                    if gcol is None:
                        cp(cengs[i % 3], sbf.t[:, 0:nch * n], sg.t[:, 0:nch * n], [sg], [sbf])
                    else:
                        for c in range(nch):
                            ts(DVE, sbv[:, c, :], sgv[:, c, :], vecs.t[:, gcol + c:gcol + c + 1], ALU.mult,
                               [sg, vecs], [sbf])
                    dma(POOL, dst, sbv, [sbf], [], bds[s_])
                flush_phase()

            def rms_block(ph_sq, xs, n, a_col, b_col, hs, pp, sd, rstd, tmp, divisor=1.0 / D):
                act(ph_sq.t[:, :, 0:n], xs.t[:, :, 0:n], AF.Square, [xs], [ph_sq])
                for n0 in range(0, n, 512):
                    nn = min(512, n - n0)
                    p = pp.next()
                    for k in range(8):
                        mm(p.t[:, 0:nn], ones_bf.t[:, :], ph_sq.t[:, k, n0:n0 + nn], k == 0, k == 7, [ones_bf, ph_sq], [p])
                    act(sd.t[:, n0:n0 + nn], p.t[:, 0:nn], AF.Ln, [p, cst], [sd], bias=eps_ap, scale=divisor)
                act(rstd.t[:, 0:n], sd.t[:, 0:n], AF.Exp, [sd], [rstd], scale=-0.5)
                for k in range(8):
                    tm = tmp[k % len(tmp)]
                    stt(DVE, tm.t[:, 0:n], xs.t[:, k, 0:n], cst.t[:, a_col + k:a_col + k + 1], rstd.t[:, 0:n],
                        ALU.mult, ALU.mult, [xs, cst, rstd], [tm])
                    act(hs.t[:, k, 0:n], tm.t[:, 0:n], AF.Identity, [tm, cst], [hs],
                        bias=cst.t[:, b_col + k:b_col + k + 1], scale=1.0)

            with ExitStack() as ph:
                xs = [sb(ph, f"xs{i}", [128, 8, 512], F32) for i in range(2)]
                xds = [newds() for _ in range(2)]
                hs = [sb(ph, f"hs{i}", [128, 8, 512], BF16) for i in range(2)]
                hds = [newds(True) for _ in range(2)]
                sq = sb(ph, "sq", [128, 8, 512], BF16)
                sd = sb(ph, "sd", [128, 512], F32)
                rstd = sb(ph, "rstd", [128, 512], F32)
                tmp = [sb(ph, f"tmp{i}", [128, 512], F32) for i in range(2)]
                pp = Pool_([0, 1])
                for b in range(NB):
                    t0, n = blk_range(b)
                    s_ = b % 2
                    dma(SP, xs[s_].t[:, :, 0:n], xT[:, :, t0:t0 + n], [], [xs[s_]], xds[s_])
                    rms_block(sq, xs[s_], n, C_A1 if b < 8 else C_A1C, C_B1 if b < 8 else C_B1C, hs[s_], pp, sd, rstd, tmp)
                    dma(POOL, hT_s[:, :, t0:t0 + n], hs[s_].t[:, :, 0:n], [hs[s_]], [], hds[s_])
                flush_phase()

            with ExitStack() as mla:
                Kt = sb(mla, "Kt", [96, 8, T], BF16)
                Va = sb(mla, "Va", [128, 34, 8, 65], BF16)
                qcn = sb(mla, "qcn", [128, 3, L], BF16)
                with ExitStack() as ph:
                    wqc = sb(ph, "wqc", [128, 8, 640], BF16)
                    wkr = sb(ph, "wkr", [128, 8, 192], BF16)
                    wukv = sb(ph, "wukv", [128, 2, 1024], BF16)
                    dma(SP, wqc.t[:, :, :], win_s[:, :, 0:640], [], [wqc], newds())
                    dma(SP, wkr.t[:, :, :], wkr_s[:, :, :], [], [wkr], newds())
                    dma(SP, wukv.t[:, :, :], wukv_s[:, :, :], [], [wukv], newds())
                    hs = [sb(ph, f"hs{i}", [128, 8, 512], BF16) for i in range(2)]
                    hds = [newds() for _ in range(2)]
                    csb = [sb(ph, f"csb{i}", [96, 2, 512], F32) for i in range(2)]
                    cds = [newds() for _ in range(2)]
                    lat_f = sb(ph, "lat_f", [128, 5, 512], F32)
                    lat_sq = sb(ph, "lat_sq", [128, 5, 512], BF16)
                    kvn = sb(ph, "kvn", [128, 2, 512], BF16)
                    sdq = sb(ph, "sdq", [128, 512], F32)
                    rsq = sb(ph, "rsq", [128, 512], F32)
                    sdk = sb(ph, "sdk", [128, 512], F32)
                    rsk = sb(ph, "rsk", [128, 512], F32)
                    t1 = sb(ph, "t1", [96, 512], F32)
                    t2 = sb(ph, "t2", [96, 512], F32)
                    kro = sb(ph, "kro", [96, 512], BF16)
                    mset(DVE, Va.t[:, :, :, 64:65], 1.0, [Va])
                    pp = Pool_([0, 1, 2, 3, 4, 5, 6, 7])
                    ev = [ACT, DVE]
                    evi = 0
                    for b in range(NB):
                        t0, n = blk_range(b)
                        s_ = b % 2
                        h = hs[s_]
                        dma(SP, h.t[:, :, 0:n], hT_s[:, :, t0:t0 + n], [], [h], hds[s_])
                        if b < 8:
                            dma(SP, csb[s_].t[64:96, :, :], cs_d[64:96, :, t0:t0 + n], [], [csb[s_]], cds[s_])
                        chunks = ([0, 1, 2] if b < 8 else []) + [3, 4]
                        for c in chunks:
                            p = pp.next()
                            for k in range(8):
                                mm(p.t[:, 0:n], wqc.t[:, k, c * 128:(c + 1) * 128], h.t[:, k, 0:n], k == 0, k == 7,
                                   [wqc, h], [p])
                            act(lat_sq.t[:, c, 0:n], p.t[:, 0:n], AF.Square, [p], [lat_sq])
                            cp(DVE, lat_f.t[:, c, 0:n], p.t[:, 0:n], [p], [lat_f])
                        if b < 8:
                            p = pp.next()
                            for c in range(3):
                                mm(p.t[:, 0:n], ones_bf.t[:, :], lat_sq.t[:, c, 0:n], c == 0, c == 2, [ones_bf, lat_sq], [p])
                            act(sdq.t[:, 0:n], p.t[:, 0:n], AF.Ln, [p, cst], [sdq], bias=eps_ap, scale=1.0 / 384)
                            act(rsq.t[:, 0:n], sdq.t[:, 0:n], AF.Exp, [sdq], [rsq], scale=-0.5)
                            for c in range(3):
                                tt(DVE, qcn.t[:, c, t0:t0 + n], lat_f.t[:, c, 0:n], rsq.t[:, 0:n], ALU.mult,
                                   [lat_f, rsq], [qcn])
                        p = pp.next()
                        for c in range(2):
                            mm(p.t[:, 0:n], ones_bf.t[:, :], lat_sq.t[:, 3 + c, 0:n], c == 0, c == 1, [ones_bf, lat_sq], [p])
                        act(sdk.t[:, 0:n], p.t[:, 0:n], AF.Ln, [p, cst], [sdk], bias=eps_ap, scale=1.0 / 256)
                        act(rsk.t[:, 0:n], sdk.t[:, 0:n], AF.Exp, [sdk], [rsk], scale=-0.5)
                        for c in range(2):
                            tt(DVE, kvn.t[:, c, 0:n], lat_f.t[:, 3 + c, 0:n], rsk.t[:, 0:n], ALU.mult, [lat_f, rsk], [kvn])
                        for hh in range(8):
                            p = pp.next()
                            for c in range(2):
                                mm(p.t[0:64, 0:n], wukv.t[:, c, hh * 64:(hh + 1) * 64], kvn.t[:, c, 0:n], c == 0, c == 1,
                                   [wukv, kvn], [p])
                            cp(ev[evi % 2], Kt.t[0:64, hh, t0:t0 + n], p.t[0:64, 0:n], [p], [Kt])
                            evi += 1
                        for tl in range(n // 128):
                            kt = (t0 + tl * 128) // 128
                            p = pp.next()
                            for c in range(2):
                                mm(p.t[:, 0:512], kvn.t[:, c, tl * 128:(tl + 1) * 128], wukv.t[:, c, 512:1024],
                                   c == 0, c == 1, [wukv, kvn], [p])
                            cp(ev[evi % 2], Va.t[:, kt, :, 0:64], p.t[:, 0:512].rearrange("p (h d) -> p h d", d=64), [p], [Va])
                            evi += 1
                        pk = pp.next()
                        for k in range(8):
                            mm(pk.t[0:96, 0:n], wkr.t[:, k, 0:96], h.t[:, k, 0:n], k == 0, k == 7, [wkr, h], [pk])
                        if b < 8:
                            pks = pp.next()
                            for k in range(8):
                                mm(pks.t[0:96, 0:n], wkr.t[:, k, 96:192], h.t[:, k, 0:n], k == 0, k == 7, [wkr, h], [pks])
                            tt(DVE, t1.t[64:96, 0:n], pk.t[64:96, 0:n], csb[s_].t[64:96, 0, 0:n], ALU.mult, [pk, csb[s_]], [t1])
                            tt(DVE, t2.t[64:96, 0:n], pks.t[64:96, 0:n], csb[s_].t[64:96, 1, 0:n], ALU.mult, [pks, csb[s_]], [t2])
                            tt(POOL, kro.t[64:96, 0:n], t1.t[64:96, 0:n], t2.t[64:96, 0:n], ALU.add, [t1, t2], [kro])
                        else:
                            cp(ACT, kro.t[64:96, 0:n], pk.t[64:96, 0:n], [pk], [kro])
                        for hh in range(8):
                            cp(POOL, Kt.t[64:96, hh, t0:t0 + n], kro.t[64:96, 0:n], [kro], [Kt])
                    flush_phase()

                with ExitStack() as ph:
                    wuq = sb(ph, "wuq", [128, 3, 768], BF16)
                    wuqs = sb(ph, "wuqs", [128, 3, 768], BF16)
                    dma(SP, wuq.t[:, :, :], wuq_s[:, :, :], [], [wuq], newds())
                    dma(SP, wuqs.t[:, :, :], wuqs_s[:, :, :], [], [wuqs], newds())
                    csb = [sb(ph, f"csb{i}", [96, 2, 512], F32) for i in range(2)]
                    cds = [newds() for _ in range(2)]
                    Qt = [sb(ph, f"Qt{i}", [96, 512], BF16) for i in range(2)]
                    Pt = [sb(ph, f"Pt{i}", [128, 1024], BF16) for i in range(4)]
                    t1 = sb(ph, "t1", [96, 512], F32)
                    t2 = sb(ph, "t2", [96, 512], F32)
                    osb = [sb(ph, f"osb{i}", [65, 512], F32) for i in range(2)]
                    bcs = sb(ph, "bcs", [64, 512], F32)
                    ost = [sb(ph, f"ost{i}", [64, 512], BF16) for i in range(2)]
                    ods = [newds(True) for _ in range(2)]
                    pO_ = PS[6]
                    pQB = PS[7]
                    prc = 0
                    stgC = [sb(ph, f"stgC{i}", [128, 4096], F32) for i in range(2)]
                    stbC = [sb(ph, f"stbC{i}", [128, 4096], BF16) for i in range(1)]
                    sdsC = [newds() for _ in range(2)]
                    bdsC = [newds(True) for _ in range(1)]
                    late = list(late_pieces)
                    cstate = {"pend": None, "k": 0}

                    def cast_step():
                        if cstate["pend"] is not None:
                            slot, (src, dst, nch, n, gcol, np_) = cstate["pend"]
                            sg, sbf = stgC[slot], stbC[0]
                            cp(DVE, sbf.t[0:np_, 0:nch * n], sg.t[0:np_, 0:nch * n], [sg], [sbf])
                            if np_ == 128:
                                sbv = sbf.t[:, 0:nch * n].rearrange("p (c n) -> p c n", n=n)
                            else:
                                sbv = sbf.t[0:np_, 0:n]
                            dma(POOL, dst, sbv, [sbf], [], bdsC[0])
                            cstate["pend"] = None
                        if late:
                            piece = late.pop(0)
                            src, dst, nch, n, gcol, np_ = piece
                            slot = cstate["k"] % 2
                            cstate["k"] += 1
                            sg = stgC[slot]
                            if np_ == 128:
                                sgv = sg.t[:, 0:nch * n].rearrange("p (c n) -> p c n", n=n)
                            else:
                                sgv = sg.t[0:np_, 0:n]
                            dma(SP, sgv, src, [], [sg], sdsC[slot])
                            cstate["pend"] = (slot, piece)

                    def qgen(it_):
                        qb_, hh_ = it_ // 8, it_ % 8
                        q0_ = qb_ * 512
                        cs_ = csb[qb_ % 2]
                        if hh_ == 0:
                            dma(SP, cs_.t[64:96, :, :], cs_d[64:96, :, q0_:q0_ + 512], [], [cs_], cds[qb_ % 2])
                        Q = Qt[it_ % 2]
                        for hf in range(2):
                            cl = slice(hf * 256, (hf + 1) * 256)
                            qc = slice(q0_ + hf * 256, q0_ + (hf + 1) * 256)
                            for c in range(3):
                                mm(pQB.t[0:96, 0:256], wuq.t[:, c, hh_ * 96:(hh_ + 1) * 96], qcn.t[:, c, qc],
                                   c == 0, c == 2, [wuq, qcn], [pQB])
                            for c in range(3):
                                mm(pQB.t[0:96, 256:512], wuqs.t[:, c, hh_ * 96:(hh_ + 1) * 96], qcn.t[:, c, qc],
                                   c == 0, c == 2, [wuqs, qcn], [pQB])
                            cp(DVE, Q.t[0:64, cl], pQB.t[0:64, 0:256], [pQB], [Q])
                            tt(DVE, t1.t[64:96, cl], pQB.t[64:96, 0:256], cs_.t[64:96, 0, cl], ALU.mult, [pQB, cs_], [t1])
                            tt(DVE, t2.t[64:96, cl], pQB.t[64:96, 256:512], cs_.t[64:96, 1, cl], ALU.mult, [pQB, cs_], [t2])
                        tt(POOL, Q.t[64:96, :], t1.t[64:96, :], t2.t[64:96, :], ALU.add, [t1, t2], [Q])

                    def norm_tail(it_):
                        qb_, hh_ = it_ // 8, it_ % 8
                        q0_ = qb_ * 512
                        ob = osb[it_ % 2]
                        mm(pQB.t[0:64, :], ones_f.t[64:65, 0:64], ob.t[64:65, :], True, True, [ones_f, ob], [pQB])
                        cp(DVE, bcs.t[:, :], pQB.t[0:64, :], [pQB], [bcs])
                        o_ = ost[it_ % 2]
                        tt(DVE, o_.t[:, :], ob.t[0:64, :], bcs.t[:, :], ALU.mult, [ob, bcs], [o_])
                        hp = (hh_ % 2) * 64
                        dma(POOL, om_s[hp:hp + 64, hh_ // 2, q0_:q0_ + 512], o_.t[:, :], [o_], [], ods[it_ % 2])

                    NKP = 17

                    def s_step(hh_, Q, kp):
                        nonlocal_prc = prcbox
                        pi = nonlocal_prc[0] % 3
                        P = Pt[nonlocal_prc[0] % 4]
                        nonlocal_prc[0] += 1
                        b0, b1 = PS[2 * pi], PS[2 * pi + 1]
                        for u, bk in ((0, b0), (1, b1)):
                            kt = 2 * kp + u
                            mm(bk.t[:, :], Kt.t[0:96, hh_, kt * 128:(kt + 1) * 128], Q.t[0:96, :], True, True,
                               [Kt, Q], [bk])
                        act(P.t[:, :], PP[pi][:, :], AF.Exp, [b0, b1], [P], scale=MLA_SCALE)
                        return P

                    prcbox = [0]
                    qgen(0)
                    for it in range(64):
                        qb, hh = it // 8, it % 8
                        Q = Qt[it % 2]
                        Ps = {}
                        Ps[0] = s_step(hh, Q, 0)
                        Ps[1] = s_step(hh, Q, 1)
                        for kp in range(NKP):
                            if kp + 2 < NKP:
                                Ps[kp + 2] = s_step(hh, Q, kp + 2)
                            Pp = Ps.pop(kp)
                            for u in range(2):
                                kt = 2 * kp + u
                                mm(pO_.t[0:65, :], Va.t[:, kt, hh, 0:65], Pp.t[:, u * 512:(u + 1) * 512],
                                   kt == 0, kt == 33, [Va, Pp], [pO_])
                            if kp == 3 and it >= 1:
                                norm_tail(it - 1)
                            if kp == 8 and it + 1 < 64:
                                qgen(it + 1)
                        ob = osb[it % 2]
                        cp(DVE, ob.t[0:65, :], pO_.t[0:65, :], [pO_], [ob])
                        recip(ob.t[64:65, :], ob.t[64:65, :], [ob], [ob])
                        cast_step()
                    norm_tail(63)
                    while late or cstate["pend"] is not None:
                        cast_step()
                    flush_phase()

            with ExitStack() as ph:
                wg = sb(ph, "wg", [128, 8, 2080], BF16)
                wdec = sb(ph, "wdec", [17, 1024], BF16)
                tri = sb(ph, "tri", [128, 4, 128], F32)
                dma(SP, wg.t[:, :, :], win_s[:, :, O_GQ:O_GQ + 2080], [], [wg], newds())
                dma(SP, wdec.t[:, :], wdec_s[:, :], [], [wdec], newds())
                dma(SP, tri.t[:, :, :], tri_d[:, 0:4, :], [], [tri], newds())
                TRI_F, TRIS_F, TRI_B, TRIS_B = 0, 1, 2, 3
                Sf_in = sb(ph, "Sf_in", [128, 32, 512], BF16)
                Sst = [sb(ph, f"Sst{i}", [128, 512], F32) for i in range(2)]
                Sb_bf = sb(ph, "Sb_bf", [128, 512], BF16)
                hs = [sb(ph, f"hs{i}", [128, 8, 512], BF16) for i in range(2)]
                hds = [newds() for _ in range(2)]
                lfa = [[sb(ph, f"lfa{d}{i}", [32, 128], BF16) for i in range(2)] for d in range(2)]
                e_ = [[sb(ph, f"e{d}{i}", [128, 512], F32) for i in range(2)] for d in range(2)]
                Lt = [[sb(ph, f"L{d}{i}", [128, 512], F32) for i in range(2)] for d in range(2)]
                E3 = [sb(ph, f"E3{i}", [128, 512], F32) for i in range(2)]
                dec = [sb(ph, f"dec{i}", [128, 4], F32) for i in range(2)]
                kd = [sb(ph, f"kd{i}", [128, 512], BF16) for i in range(2)]
                vb = [sb(ph, f"vb{i}", [128, 512], BF16) for i in range(2)]
                Eq = [[sb(ph, f"Eq{d}{i}", [128, 512], F32) for i in range(2)] for d in range(2)]
                Ek = [[sb(ph, f"Ek{d}{i}", [128, 512], F32) for i in range(2)] for d in range(2)]
                qe = [[sb(ph, f"qe{d}{i}", [128, 512], BF16) for i in range(2)] for d in range(2)]
                ke = [[sb(ph, f"ke{d}{i}", [128, 512], BF16) for i in range(2)] for d in range(2)]
                Amf = sb(ph, "Amf", [128, 512], F32)
                Amb = sb(ph, "Amb", [128, 512], F32)
                Amt = [sb(ph, f"Amt{i}", [128, 512], BF16) for i in range(2)]
                gsq = sb(ph, "gsq", [128, 512], BF16)
                gsd = sb(ph, "gsd", [128, 512], F32)
                grs = sb(ph, "grs", [128, 512], F32)
                qTb = [sb(ph, f"qTb{i}", [128, 4, 512], F32) for i in range(2)]
                kTb = [sb(ph, f"kTb{i}", [128, 4, 512], F32) for i in range(2)]
                srb = [sb(ph, f"srb{i}", [128, 4, 512], F32) for i in range(2)]
                fblk = {"b": None, "slot": -1}
                fblk_of = {}
                y1 = sb(ph, "y1", [128, 512], F32)
                ys = [sb(ph, f"ys{i}", [128, 4, 128], BF16) for i in range(2)]
                yds = [newds(True) for _ in range(2)]
                for d in range(2):
                    for i in range(2):
                        mset(DVE, lfa[d][i].t[:, :], 1.0, [lfa[d][i]])
                mset(DVE, Sst[0].t[:, :], 0.0, [Sst[0]])
                mset(DVE, Sst[1].t[:, :], 0.0, [Sst[1]])
                mset(POOL, Sb_bf.t[:, :], 0.0, [Sb_bf])
                pp = Pool_([4, 5, 6, 7])
                pKV = Pool_([0, 1, 2, 3])
                cnt = [0]
                hload = [0]
                cur_blk = [None, None]

                def get_h(n_):
                    b = n_ // 4 if n_ < 32 else 8
                    if cur_blk[0] != b:
                        s_ = hload[0] % 2
                        hload[0] += 1
                        t0, n = blk_range(b)
                        dma(SP, hs[s_].t[:, :, 0:n], hT_s[:, :, t0:t0 + n], [], [hs[s_]], hds[s_])
                        cur_blk[0] = b
                        cur_blk[1] = hs[s_]
                    off = (n_ % 4) * 128 if n_ < 32 else (n_ - 32) * 128
                    return cur_blk[1], off

                def gate_proj(h, off, d, i):
                    p = pp.next()
                    for k in range(8):
                        mm(p.t[0:16, 0:128], wg.t[:, k, 2048 + d * 16:2064 + d * 16], h.t[:, k, off:off + 128],
                           k == 0, k == 7, [wg, h], [p])
                    la = lfa[d][i]
                    cp(ACT, la.t[0:16, :], p.t[0:16, 0:128], [p], [la])

                def gate_x(d, i):
                    la = lfa[d][i]
                    px = pp.next()
                    mm(px.t[:, :], la.t[0:17, :], wdec.t[0:17, d * 512:(d + 1) * 512], True, True, [la, wdec], [px])
                    act(e_[d][i].t[:, :], px.t[:, :], AF.Exp, [px], [e_[d][i]], scale=-1.0)
                    act(Lt[d][i].t[:, :], e_[d][i].t[:, :], AF.Ln, [e_[d][i], cst], [Lt[d][i]],
                        bias=cst.t[:, C_ONE:C_ONE + 1], scale=1.0)
                    return Lt[d][i]

                def proj_tok(h, off, col0):
                    p = pKV.next()
                    for k in range(8):
                        mm(p.t[:, :], h.t[:, k, off:off + 128], wg.t[:, k, col0:col0 + 512], k == 0, k == 7, [wg, h], [p])
                    return p

                def su_prep(d, i, Lx, pk, pv):
                    tris = TRIS_F if d == 0 else TRIS_B
                    pr = pp.next()
                    mm(pr.t[:, :], tri.t[:, tris, :], Lx.t[:, :], True, True, [tri, Lx], [pr])
                    act(E3[i].t[:, :], pr.t[:, :], AF.Exp, [pr], [E3[i]], scale=-1.0 / 16)
                    pt = pp.next()
                    for hh in range(4):
                        mm(pt.t[:, hh:hh + 1], Lx.t[:, hh * 128:(hh + 1) * 128], ones_f.t[:, 0:1], True, True,
                           [Lx, ones_f], [pt])
                    act(dec[i].t[:, :], pt.t[:, 0:4], AF.Exp, [pt], [dec[i]], scale=-1.0 / 16)
                    tt(DVE, kd[i].t[:, :], pk.t[:, :], E3[i].t[:, :], ALU.mult, [pk, E3[i]], [kd[i]])
                    cp(ACT, vb[i].t[:, :], pv.t[:, :], [pv], [vb[i]])

                def su_mm(i):
                    pu = pp.next()
                    for hh in range(4):
                        mm(pu.t[:, hh * 128:(hh + 1) * 128], kd[i].t[:, hh * 128:(hh + 1) * 128],
                           vb[i].t[:, hh * 128:(hh + 1) * 128], True, True, [kd[i], vb[i]], [pu])
                    return pu

                def apply_update(d, i, pu):
                    S = Sst[d]
                    for hh in range(4):
                        sl = slice(hh * 128, (hh + 1) * 128)
                        stt(DVE, S.t[:, sl], S.t[:, sl], dec[i].t[:, hh:hh + 1], pu.t[:, sl], ALU.mult, ALU.add,
                            [S, dec[i], pu], [S])

                def A1a(n_):
                    i = cnt[0] % 2
                    cnt[0] += 1
                    h, off = get_h(n_)
                    gate_proj(h, off, 0, i)
                    pk = proj_tok(h, off, 512)
                    pv = proj_tok(h, off, 1024)
                    Lx = gate_x(0, i)
                    return (n_, i, Lx, pk, pv)

                def A1b(c):
                    n_, i, Lx, pk, pv = c
                    su_prep(0, i, Lx, pk, pv)

                def B1(c):
                    n_, i, Lx, pk, pv = c
                    if n_ < 32:
                        cp(ACT, Sf_in.t[:, n_, :], Sst[0].t[:, :], [Sst[0]], [Sf_in])
                    pu = su_mm(i)
                    apply_update(0, i, pu)

                order1 = [32, 33] + list(range(32))
                prevc = None
                for n_ in order1:
                    c = A1a(n_)
                    if prevc is not None:
                        B1(prevc)
                    A1b(c)
                    prevc = c
                B1(prevc)

                cur_blk[0] = None
                v3 = lambda t_: t_.t[:, :].rearrange("p (h c) -> p h c", c=128)

                def A2a(n_):
                    i = cnt[0] % 2
                    cnt[0] += 1
                    h, off = get_h(n_)
                    gate_proj(h, off, 1, i)
                    if n_ < 32:
                        gate_proj(h, off, 0, i)
                    pk = proj_tok(h, off, 512)
                    pv = proj_tok(h, off, 1024)
                    Lb = gate_x(1, i)
                    Lf = gate_x(0, i) if n_ < 32 else None
                    if n_ < 32:
                        b_ = n_ // 4
                        if fblk["b"] != b_:
                            fblk["b"] = b_
                            fblk["slot"] += 1
                            fs = fblk["slot"] % 2
                            evs = [ACT, DVE]
                            ei = 0
                            for col0, dst, fn_ in ((0, qTb[fs], None), (512, kTb[fs], None), (1536, srb[fs], AF.Silu)):
                                for hh in range(4):
                                    p = pp.next()
                                    for k in range(8):
                                        mm(p.t[:, :], wg.t[:, k, col0 + hh * 128:col0 + (hh + 1) * 128], h.t[:, k, 0:512],
                                           k == 0, k == 7, [wg, h], [p])
                                    if fn_ is not None:
                                        act(dst.t[:, hh, :], p.t[:, :], fn_, [p], [dst])
                                    else:
                                        cp(evs[ei % 2], dst.t[:, hh, :], p.t[:, :], [p], [dst])
                                        ei += 1
                        fblk_of[n_] = fblk["slot"] % 2
                    return (n_, i, Lb, Lf, pk, pv, off)

                def A2b(c):
                    n_, i, Lb, Lf, pk, pv, off = c
                    if n_ < 32:
                        fs = fblk_of[n_]
                        for d, Lx, trid in ((0, Lf, TRI_F), (1, Lb, TRI_B)):
                            pbt = pp.next()
                            for hh in range(4):
                                mm(pbt.t[:, hh * 128:(hh + 1) * 128], Lx.t[:, hh * 128:(hh + 1) * 128], tri.t[:, trid, :],
                                   True, True, [Lx, tri], [pbt])
                            act(Eq[d][i].t[:, :], pbt.t[:, :], AF.Exp, [pbt], [Eq[d][i]], scale=-1.0 / 16)
                            act(Ek[d][i].t[:, :], pbt.t[:, :], AF.Exp, [pbt], [Ek[d][i]], scale=1.0 / 16)
                    su_prep(1, i, Lb, pk, pv)
                    if n_ < 32:
                        for d in range(2):
                            stt(DVE, v3(qe[d][i]), qTb[fs].t[:, :, off:off + 128], GLA_QSCALE, v3(Eq[d][i]), ALU.mult,
                                ALU.mult, [qTb[fs], Eq[d][i]], [qe[d][i]])
                        for d in range(2):
                            tt(DVE, v3(ke[d][i]), kTb[fs].t[:, :, off:off + 128], v3(Ek[d][i]), ALU.mult,
                               [kTb[fs], Ek[d][i]], [ke[d][i]])

                def B2(c):
                    n_, i, Lb, Lf, pk, pv, off = c
                    if n_ >= 32:
                        pu = su_mm(i)
                        apply_update(1, i, pu)
                        cp(ACT, Sb_bf.t[:, :], Sst[1].t[:, :], [Sst[1]], [Sb_bf])
                        return
                    pa = []
                    for d in range(2):
                        p = pp.next()
                        for hh in range(4):
                            sl = slice(hh * 128, (hh + 1) * 128)
                            mm(p.t[:, sl], ke[d][i].t[:, sl], qe[d][i].t[:, sl], True, True, [ke[d][i], qe[d][i]], [p])
                        pa.append(p)
                    for hh in range(4):
                        sl = slice(hh * 128, (hh + 1) * 128)
                        tt(DVE, Amf.t[:, sl], pa[0].t[:, sl], tri.t[:, TRI_F, :], ALU.mult, [pa[0], tri], [Amf])
                        tt(DVE, Amb.t[:, sl], pa[1].t[:, sl], tri.t[:, TRI_B, :], ALU.mult, [pa[1], tri], [Amb])
                    tt(DVE, Amt[i].t[:, :], Amf.t[:, :], Amb.t[:, :], ALU.add, [Amf, Amb], [Amt[i]])
                    pu = su_mm(i)
                    po = pp.next()
                    for hh in range(4):
                        sl = slice(hh * 128, (hh + 1) * 128)
                        mm(po.t[:, sl], Sf_in.t[:, n_, sl], qe[0][i].t[:, sl], True, False, [Sf_in, qe[0][i]], [po])
                        mm(po.t[:, sl], Sb_bf.t[:, sl], qe[1][i].t[:, sl], False, False, [Sb_bf, qe[1][i]], [po])
                        mm(po.t[:, sl], vb[i].t[:, sl], Amt[i].t[:, sl], False, True, [vb[i], Amt[i]], [po])
                    apply_update(1, i, pu)
                    cp(ACT, Sb_bf.t[:, :], Sst[1].t[:, :], [Sst[1]], [Sb_bf])
                    act(gsq.t[:, :], po.t[:, :], AF.Square, [po], [gsq])
                    pn = pp.next()
                    mm(pn.t[:, :], ones_bf.t[:, :], gsq.t[:, :], True, True, [ones_bf, gsq], [pn])
                    act(gsd.t[:, :], pn.t[:, :], AF.Ln, [pn, cst], [gsd], bias=eps_ap, scale=1.0 / 128)
                    act(grs.t[:, :], gsd.t[:, :], AF.Exp, [gsd], [grs], scale=-0.5)
                    stt(DVE, y1.t[:, :], po.t[:, :], vecs.t[:, V_GLAG:V_GLAG + 1], grs.t[:, :], ALU.mult, ALU.mult,
                        [po, vecs, grs], [y1])
                    y_ = ys[i]
                    fsb = fblk_of[n_]
                    tt(DVE, y_.t[:, :, :], v3(y1), srb[fsb].t[:, :, off:off + 128], ALU.mult, [y1, srb[fsb]], [y_])
                    dma(POOL, yT_s[:, :, n_ * 128:(n_ + 1) * 128], y_.t[:, :, :], [y_], [], yds[i])

                order2 = [33, 32] + list(range(31, -1, -1))
                prevc = None
                for n_ in order2:
                    c = A2a(n_)
                    if prevc is not None:
                        B2(prevc)
                    A2b(c)
                    prevc = c
                B2(prevc)
                flush_phase()

            with ExitStack() as ph:
                wgt = sb(ph, "wgt", [128, 8, 2048], BF16)
                wbrm = sb(ph, "wbrm", [128, 4, D], BF16)
                wbrg = sb(ph, "wbrg", [128, 4, D], BF16)
                wout = sb(ph, "wout", [128, 8, D], BF16)
                dma(SP, wgt.t[:, :, :], win_s[:, :, O_GATE:O_GATE + 2048], [], [wgt], newds())
                dma(SP, wbrm.t[:, :, :], wbrm_s[:, :, :], [], [wbrm], newds())
                dma(SP, wbrg.t[:, :, :], wbrg_s[:, :, :], [], [wbrg], newds())
                dma(SP, wout.t[:, :, :], wout_s[:, :, :], [], [wout], newds())
                hs = [sb(ph, f"hs{i}", [128, 8, 512], BF16) for i in range(2)]
                om = [sb(ph, f"om{i}", [128, 4, 512], BF16) for i in range(2)]
                yb = [sb(ph, f"yb{i}", [128, 4, 512], BF16) for i in range(2)]
                xs = [sb(ph, f"xs{i}", [128, 8, 512], F32) for i in range(2)]
                lds = [[newds() for _ in range(4)] for _ in range(2)]
                mT = sb(ph, "mT", [128, 8, 512], BF16)
                x1 = [sb(ph, f"x1{i}", [128, 8, 512], F32) for i in range(2)]
                xds = [newds(True) for _ in range(2)]
                sgm = [sb(ph, f"sgm{i}", [128, 512], F32) for i in range(2)]
                sgg = [sb(ph, f"sgg{i}", [128, 512], F32) for i in range(2)]
                u1 = [sb(ph, f"u1{i}", [128, 512], F32) for i in range(2)]
                u2 = [sb(ph, f"u2{i}", [128, 512], F32) for i in range(2)]
                pp = Pool_([0, 1, 2, 3, 4, 5, 6, 7])
                for b in range(8):
                    t0 = b * 512
                    s_ = b % 2
                    h, o_, y_, x_ = hs[s_], om[s_], yb[s_], xs[s_]
                    dma(SP, h.t[:, :, :], hT_s[:, :, t0:t0 + 512], [], [h], lds[s_][0])
                    dma(SP, o_.t[:, :, :], om_s[:, :, t0:t0 + 512], [], [o_], lds[s_][1])
                    dma(SP, y_.t[:, :, :], yT_s[:, :, t0:t0 + 512], [], [y_], lds[s_][2])
                    dma(SP, x_.t[:, :, :], xT[:, :, t0:t0 + 512], [], [x_], lds[s_][3])
                    for fc in range(8):
                        i = fc % 2
                        cs = slice(fc * 128, (fc + 1) * 128)
                        pgm = pp.next()
                        for k in range(8):
                            mm(pgm.t[:, :], wgt.t[:, k, fc * 128:(fc + 1) * 128], h.t[:, k, :], k == 0, k == 7, [wgt, h], [pgm])
                        pgg = pp.next()
                        for k in range(8):
                            mm(pgg.t[:, :], wgt.t[:, k, 1024 + fc * 128:1024 + (fc + 1) * 128], h.t[:, k, :], k == 0, k == 7,
                               [wgt, h], [pgg])
                        pbm = pp.next()
                        for k in range(4):
                            mm(pbm.t[:, :], wbrm.t[:, k, cs], o_.t[:, k, :], k == 0, k == 3, [wbrm, o_], [pbm])
                        pbg = pp.next()
                        for k in range(4):
                            mm(pbg.t[:, :], wbrg.t[:, k, cs], y_.t[:, k, :], k == 0, k == 3, [wbrg, y_], [pbg])
                        act(sgm[i].t[:, :], pgm.t[:, :], AF.Sigmoid, [pgm], [sgm[i]])
                        act(sgg[i].t[:, :], pgg.t[:, :], AF.Sigmoid, [pgg], [sgg[i]])
                        tt(DVE, u1[i].t[:, :], pbm.t[:, :], sgm[i].t[:, :], ALU.mult, [pbm, sgm[i]], [u1[i]])
                        tt(DVE, u2[i].t[:, :], pbg.t[:, :], sgg[i].t[:, :], ALU.mult, [pbg, sgg[i]], [u2[i]])
                        tt(POOL, mT.t[:, fc, :], u1[i].t[:, :], u2[i].t[:, :], ALU.add, [u1[i], u2[i]], [mT])
                    xo = x1[s_]
                    for fc in range(8):
                        pm = pp.next()
                        for k in range(8):
                            mm(pm.t[:, :], wout.t[:, k, fc * 128:(fc + 1) * 128], mT.t[:, k, :], k == 0, k == 7, [wout, mT], [pm])
                        stt(DVE, xo.t[:, fc, :], pm.t[:, :], cst.t[:, C_GT1 + fc:C_GT1 + fc + 1], x_.t[:, fc, :],
                            ALU.mult, ALU.add, [pm, cst, x_], [xo])
                    dma(POOL, x1_s[:, :, t0:t0 + 512], xo.t[:, :, :], [xo], [], xds[s_])
                flush_phase()

            with ExitStack() as ph:
                wup = sb(ph, "wup", [128, 8, 2 * DFF], BF16)
                wvb, wgb = [], []
                for q in range(4):
                    c0, c1 = q * 768, min((q + 1) * 768, DFF)
                    bv, bg = Buf(f"wupv{q}"), Buf(f"wupg{q}")
                    allbufs.extend([bv, bg])
                    wvb.append(bv)
                    wgb.append(bg)
                    dma(SP, wup.t[:, :, c0:c1], wup_s[:, :, c0:c1], [], [bv], newds())
                    dma(SP, wup.t[:, :, DFF + c0:DFF + c1], wup_s[:, :, DFF + c0:DFF + c1], [], [bg], newds())
                wdn = [sb(ph, f"wdn{i}", [128, NJ, 128], BF16) for i in range(2)]
                wnds = [newds() for _ in range(2)]
                x1e = [sb(ph, f"x1e{i}", [128, 8, 640], F32) for i in range(2)]
                xds = [newds() for _ in range(2)]
                h2s = [sb(ph, f"h2{i}", [128, 8, 640], BF16) for i in range(2)]
                sqk = [sb(ph, f"sqk{i}", [128, 640], BF16) for i in range(2)]
                sd = sb(ph, "sd", [128, 640], F32)
                tmpf = sb(ph, "tmpf", [128, 640], F32)
                aT = sb(ph, "aT", [128, NJ, 512], BF16)
                gsb = [sb(ph, f"gsb{i}", [128, 10, 66], BF16) for i in range(2)]
                identF = sb(ph, "identF", [128, 128], F32)
                dma(SP, identF.t[:, :], tri_d[:, 4, :], [], [identF], newds())
                dg = [sb(ph, f"dg{i}", [128, 9, 128], BF16) for i in range(2)]
                gl = [sb(ph, f"gl{i}", [128, 512], F32) for i in range(2)]
                ods = [newds(True) for _ in range(2)]
                for g in gsb:
                    mset(POOL, g.t[:, :, :], 0.0, [g])
                pG = Pool_([0, 1, 2, 3])
                pV = Pool_([4, 5])
                pC = Pool_([6])
                pX = Pool_([7])
                it = 0
                wdn_it = 0

                def geom(b):
                    t0 = b * 512
                    lo = max(t0 - 64, 0)
                    hi = min(t0 + 576, L)
                    return t0, lo, hi, hi - lo, t0 - lo

                def load_f(b):
                    t0, lo, hi, ne, off = geom(b)
                    dma(SP, x1e[b % 2].t[:, :, 0:ne], x1_s[:, :, lo:hi], [], [x1e[b % 2]], xds[b % 2])

                def sumsq_rstd(src_fn, n):
                    p0, p1 = PS[7], PS[6]
                    n0 = min(n, 512)
                    for k in range(8):
                        q = sqk[k % 2]
                        act(q.t[:, 0:n], src_fn(k), AF.Square, src_fn.bufs, [q])
                        mm(p0.t[:, 0:n0], ones_bf.t[:, :], q.t[:, 0:n0], k == 0, k == 7, [ones_bf, q], [p0])
                        if n > 512:
                            mm(p1.t[:, 0:n - 512], ones_bf.t[:, :], q.t[:, 512:n], k == 0, k == 7, [ones_bf, q], [p1])
                    act(sd.t[:, 0:n0], p0.t[:, 0:n0], AF.Ln, [p0, cst], [sd], bias=eps_ap, scale=1.0 / D)
                    if n > 512:
                        act(sd.t[:, 512:n], p1.t[:, 0:n - 512], AF.Ln, [p1, cst], [sd], bias=eps_ap, scale=1.0 / D)
                    act(sd.t[:, 0:n], sd.t[:, 0:n], AF.Exp, [sd], [sd], scale=-0.5)

                def rms_f(b):
                    t0, lo, hi, ne, off = geom(b)
                    xs_, hs_ = x1e[b % 2], h2s[b % 2]
                    fn = lambda k: xs_.t[:, k, 0:ne]
                    fn.bufs = [xs_]
                    sumsq_rstd(fn, ne)
                    for k in range(8):
                        stt(DVE, tmpf.t[:, 0:ne], xs_.t[:, k, 0:ne], cst.t[:, C_A2 + k:C_A2 + k + 1], sd.t[:, 0:ne],
                            ALU.mult, ALU.mult, [xs_, cst, sd], [tmpf])
                        act(hs_.t[:, k, 0:ne], tmpf.t[:, 0:ne], AF.Identity, [tmpf, cst], [hs_],
                            bias=cst.t[:, C_B2 + k:C_B2 + k + 1], scale=1.0)

                def conv_tail(jj, gg, pvv, dgg):
                    pc = pC.next()
                    for tap in range(9):
                        ky, kx = tap // 3, tap % 3
                        mm(pc.t[:, 0:512].rearrange("p (r c) -> p r c", c=64), dgg.t[:, tap, :],
                           gg.t[:, ky:ky + 8, kx:kx + 64], tap == 0, tap == 8, [dgg, gg], [pc])
                    glt = gl[jj % 2]
                    act(glt.t[:, :], pc.t[:, 0:512], AF.Gelu, [pc, vecs], [glt],
                        bias=vecs.t[:, V_CB + jj:V_CB + jj + 1], scale=1.0)
                    tt(DVE, aT.t[:, jj, :], glt.t[:, :], pvv.t[:, :], ALU.mult, [glt, pvv], [aT])

                load_f(0)
                rms_f(0)
                for b in range(8):
                    t0, lo, hi, ne, off = geom(b)
                    r_first = 1 if b == 0 else 0
                    nrows = ne // 64
                    xs_, h2 = x1e[b % 2], h2s[b % 2]
                    if b + 1 < 8:
                        load_f(b + 1)
                    if b == 7:
                        for g in gsb:
                            mset(POOL, g.t[:, 9, :], 0.0, [g])
                    prev = None
                    for j in range(NJ):
                        g = gsb[it % 2]
                        dgs = dg[it % 2]
                        for tap in range(9):
                            ts(DVE, dgs.t[:, tap, :], identF.t[:, :], vecs.t[:, V_CW + j * 9 + tap:V_CW + j * 9 + tap + 1],
                               ALU.mult, [identF, vecs], [dgs])
                        pg0 = pG.next()
                        pg1 = pG.next()
                        gc = slice(DFF + j * 128, DFF + (j + 1) * 128)
                        for k in range(8):
                            mm(pg0.t[:, 0:512], wup.t[:, k, gc], h2.t[:, k, 0:512], k == 0, k == 7, [wgb[j // 6], h2], [pg0])
                        n1 = ne - 512
                        for k in range(8):
                            mm(pg1.t[:, 0:n1], wup.t[:, k, gc], h2.t[:, k, 512:ne], k == 0, k == 7, [wgb[j // 6], h2], [pg1])
                        pv = pV.next()
                        for k in range(8):
                            mm(pv.t[:, :], wup.t[:, k, j * 128:(j + 1) * 128], h2.t[:, k, off:off + 512], k == 0, k == 7,
                               [wvb[j // 6], h2], [pv])
                        cp(ACT, g.t[:, r_first:r_first + 8, 1:65], pg0.t[:, 0:512].rearrange("p (r c) -> p r c", c=64), [pg0], [g])
                        cp(ACT, g.t[:, r_first + 8:r_first + nrows, 1:65],
                           pg1.t[:, 0:n1].rearrange("p (r c) -> p r c", c=64), [pg1], [g])
                        if prev is not None:
                            conv_tail(*prev)
                        prev = (j, g, pv, dgs)
                        it += 1
                    conv_tail(*prev)
                    if b + 1 < 8:
                        rms_f(b + 1)
                    for fc in range(8):
                        wn = wdn[wdn_it % 2]
                        dma(SP, wn.t[:, :, :], wdn_s[:, :, fc * 128:(fc + 1) * 128], [], [wn], wnds[wdn_it % 2])
                        wdn_it += 1
                        pd = pX.next()
                        for j in range(NJ):
                            mm(pd.t[:, :], wn.t[:, j, :], aT.t[:, j, :], j == 0, j == NJ - 1, [wn, aT], [pd])
                        stt(DVE, xs_.t[:, fc, off:off + 512], pd.t[:, :], cst.t[:, C_GT2 + fc:C_GT2 + fc + 1],
                            xs_.t[:, fc, off:off + 512], ALU.mult, ALU.add, [pd, cst, xs_], [xs_])
                    fn = lambda k, xs_=xs_, off=off: xs_.t[:, k, off:off + 512]
                    fn.bufs = [xs_]
                    sumsq_rstd(fn, 512)
                    for fc in range(8):
                        stt(DVE, xs_.t[:, fc, off:off + 512], xs_.t[:, fc, off:off + 512], vecs.t[:, V_FG + fc:V_FG + fc + 1],
                            sd.t[:, 0:512], ALU.mult, ALU.mult, [xs_, vecs, sd], [xs_])
                    dma(POOL, outT[:, :, t0:t0 + 512], xs_.t[:, :, off:off + 512], [xs_], [], ods[b % 2])
                flush_phase()

        try:
            build_phases()
        except StopBuild:
            pass
    return nc


_NC_CACHE = {}


def _perm32():
    return np.array(list(range(8, 16)) + list(range(0, 8)) + list(range(24, 32)) + list(range(16, 24)))


def prep_shared(inp):
    f = np.float32
    w_in = np.ascontiguousarray(inp["w_in"][0], dtype=f)
    kr = w_in[:, O_KR:O_KR + 32]
    z64 = np.zeros((D, 64), f)
    wkr2 = np.ascontiguousarray(np.concatenate([z64, kr, z64, kr[:, _perm32()]], axis=1))
    w_uq = np.ascontiguousarray(inp["w_uq"][0], dtype=f)
    idx = np.arange(768).reshape(8, 96).copy()
    idx[:, 64:] = idx[:, 64:][:, _perm32()]
    w_uqs = np.ascontiguousarray(w_uq[:, idx.reshape(-1)])
    w_ukv = inp["w_ukv"][0].reshape(256, 8, 128)
    w_ukv_p = np.ascontiguousarray(
        np.concatenate([w_ukv[:, :, :64].reshape(256, 512), w_ukv[:, :, 64:].reshape(256, 512)], axis=1), dtype=f)
    wdec = np.ascontiguousarray(np.concatenate(
        [np.concatenate([inp["gla_w_decay"][0, d], inp["gla_b_decay"][0, d][None, :]], axis=0) for d in range(2)],
        axis=1), dtype=f)
    s = np.arange(128)[:, None]
    c = np.arange(128)[None, :]
    tri = np.stack([(s <= c), (s > c), (s >= c), (s < c), (s == c)], axis=1).astype(f)
    t = np.arange(L)
    row = (t // 64).astype(f)
    col = (t % 64).astype(f)
    inv = (10000.0 ** (-np.arange(0, 16, 2, dtype=f) / 16)).astype(f)
    ar = row[None, :] * inv[:, None]
    ac = col[None, :] * inv[:, None]
    cosf = np.concatenate([np.cos(ar), np.cos(ar), np.cos(ac), np.cos(ac)], axis=0)
    sinf = np.concatenate([-np.sin(ar), np.sin(ar), -np.sin(ac), np.sin(ac)], axis=0)
    cs = np.zeros((96, 2, L), f)
    cs[64:, 0] = cosf
    cs[64:, 1] = sinf

    def pv(v, n):
        return np.asarray(v, dtype=f).reshape(n, 128).T

    shared_vec = np.zeros((128, NV), f)
    shared_vec[:, V_BADA:V_BADA + 48] = pv(inp["b_ada"][0], 48)
    shared_vec[:, V_G1:V_G1 + 8] = pv(inp["norm1_g"][0], 8)
    shared_vec[:, V_G2:V_G2 + 8] = pv(inp["norm2_g"][0], 8)
    shared_vec[:, V_FG:V_FG + 8] = pv(inp["final_g"], 8)
    shared_vec[:, V_QG:V_QG + 3] = pv(inp["q_norm_g"][0], 3)
    shared_vec[:, V_KVG:V_KVG + 2] = pv(inp["kv_norm_g"][0], 2)
    shared_vec[:, V_GLAG] = inp["gla_norm_g"][0]
    shared_vec[:, V_CB:V_CB + NJ] = pv(inp["conv_b"][0], NJ)
    shared_vec[:, V_CW:V_CW + NJ * 9] = inp["conv_w"][0].reshape(9, NJ, 128).transpose(2, 1, 0).reshape(128, NJ * 9)
    shared_vec[:, V_CC:V_CC + 8] = pv(inp["c_ctx"], 8)
    sh = {
        "w_ada": np.ascontiguousarray(inp["w_ada"][0], dtype=f),
        "w_in": w_in, "wkr2": wkr2, "w_uq": w_uq, "w_uqs": w_uqs, "w_ukv": w_ukv_p, "wdec": wdec,
        "w_brm": np.ascontiguousarray(inp["w_br_mla"][0], dtype=f),
        "w_brg": np.ascontiguousarray(inp["w_br_gla"][0], dtype=f),
        "w_out": np.ascontiguousarray(inp["w_out"][0], dtype=f),
        "w_up": np.ascontiguousarray(inp["w_up"][0], dtype=f),
        "w_dn": np.ascontiguousarray(inp["w_down"][0], dtype=f),
        "tri": tri, "cs": cs,
    }
    return sh, shared_vec


def make_in_maps(inp):
    sh, shared_vec = prep_shared(inp)
    maps = []
    for r in range(8):
        v = shared_vec.copy()
        v[:, V_C:V_C + 8] = np.asarray(inp["c"][r], np.float32).reshape(8, 128).T
        xTr = np.ascontiguousarray(
            np.concatenate([np.asarray(inp["x"][r]).T, np.asarray(inp["ctx"][r]).T], axis=1), dtype=np.float32)
        m = dict(sh)
        m["xT"] = xTr
        m["vecs"] = v
        maps.append(m)
    return maps


def kernel(**inputs):
    inp = {k: np.asarray(v) for k, v in inputs.items()}
    if "nc" not in _NC_CACHE:
        _NC_CACHE["nc"] = build(False)
    nc = _NC_CACHE["nc"]
    maps = make_in_maps(inp)
    res = run_bass_kernel_spmd(nc, maps, core_ids=list(range(8)))
    out = np.stack([np.ascontiguousarray(res.results[r]["outT"].T) for r in range(8)], axis=0)
    return out.astype(np.float32)
```

```python
import numpy as np
import ml_dtypes
from contextlib import ExitStack
import concourse.bass as bass
import concourse.mybir as mybir
from concourse.bass_utils import run_bass_kernel_spmd

F32 = mybir.dt.float32
BF16 = mybir.dt.bfloat16
AF = mybir.ActivationFunctionType
ALU = mybir.AluOpType

D = 1024
L = 4096
LC = 256
T = L + LC
NB = 9
EPS = 1e-6
MLA_SCALE = 96 ** -0.5
GLA_QSCALE = 128 ** -0.5
DFF = 2816
NJ = 22
GEN_MAX = 10000

O_QC, O_KVC, O_KR, O_GQ, O_GK, O_GV, O_GR, O_GLOW, O_GATE = 0, 384, 640, 672, 1184, 1696, 2208, 2720, 2752

V_BADA, V_G1, V_G2, V_FG, V_QG, V_KVG, V_GLAG, V_CB, V_CW, V_C, V_CC = 0, 48, 56, 64, 72, 75, 77, 78, 100, 298, 306
NV = 314


def blk_range(b):
    if b < 8:
        return b * 512, 512
    return L, LC


class Buf:
    __slots__ = ("name", "w", "r", "psum")

    def __init__(self, name):
        self.name = name
        self.w = None
        self.r = {}
        self.psum = False


class DSem:
    def __init__(self, sem):
        self.sem = sem
        self.cnt = 0


class Eng:
    def __init__(self, key, sems):
        self.key = key
        self.sems = sems
        self.gen = 0
        self.cnt = 0
        self.waited = {}
        self.prog = []

    @property
    def sem(self):
        return self.sems[self.gen]


class TT:
    def __init__(self, t, name):
        self.t = t
        self.b = Buf(name)


class StopBuild(Exception):
    pass


def build(debug=False, nphase=99):
    nc = bass.Bass("TRN2", target_bir_lowering=False)
    dkind = "ExternalOutput" if debug else "Internal"

    def din(name, shape, dt=F32):
        return nc.dram_tensor(name, list(shape), dt, kind="ExternalInput").ap()

    def dscr(name, shape, dt):
        return nc.dram_tensor(name, list(shape), dt, kind=dkind).ap()

    xT = din("xT", [D, T]).rearrange("(k p) t -> p k t", p=128)
    vecs_d = din("vecs", [128, NV])
    w_ada_d = din("w_ada", [D, 6 * D]).rearrange("(k p) n -> p k n", p=128)
    w_in_d = din("w_in", [D, 4800]).rearrange("(k p) n -> p k n", p=128)
    wkr2_d = din("wkr2", [D, 192]).rearrange("(k p) n -> p k n", p=128)
    w_uq_d = din("w_uq", [384, 768]).rearrange("(k p) n -> p k n", p=128)
    w_uqs_d = din("w_uqs", [384, 768]).rearrange("(k p) n -> p k n", p=128)
    w_ukv_d = din("w_ukv", [256, 1024]).rearrange("(k p) n -> p k n", p=128)
    wdec_d = din("wdec", [17, 1024])
    w_brm_d = din("w_brm", [512, D]).rearrange("(k p) n -> p k n", p=128)
    w_brg_d = din("w_brg", [512, D]).rearrange("(k p) n -> p k n", p=128)
    w_out_d = din("w_out", [D, D]).rearrange("(k p) n -> p k n", p=128)
    w_up_d = din("w_up", [D, 2 * DFF]).rearrange("(k p) n -> p k n", p=128)
    w_dn_d = din("w_dn", [DFF, D]).rearrange("(k p) n -> p k n", p=128)
    tri_d = din("tri", [128, 5, 128])
    cs_d = din("cs", [96, 2, L])
    outT = nc.dram_tensor("outT", [D, L], F32, kind="ExternalOutput").ap().rearrange("(k p) t -> p k t", p=128)

    win_s = dscr("win_s", [128, 8, 4800], BF16)
    wkr_s = dscr("wkr_s", [128, 8, 192], BF16)
    wuq_s = dscr("wuq_s", [128, 3, 768], BF16)
    wuqs_s = dscr("wuqs_s", [128, 3, 768], BF16)
    wukv_s = dscr("wukv_s", [128, 2, 1024], BF16)
    wdec_s = dscr("wdec_s", [17, 1024], BF16)
    wbrm_s = dscr("wbrm_s", [128, 4, D], BF16)
    wbrg_s = dscr("wbrg_s", [128, 4, D], BF16)
    wout_s = dscr("wout_s", [128, 8, D], BF16)
    wup_s = dscr("wup_s", [128, 8, 2 * DFF], BF16)
    wdn_s = dscr("wdn_s", [128, NJ, D], BF16)
    hT_s = dscr("hT_s", [128, 8, T], BF16)
    om_s = dscr("om_s", [128, 4, L], BF16)
    yT_s = dscr("yT_s", [128, 4, L], BF16)
    x1_s = dscr("x1_s", [128, 8, L], F32)

    st = ExitStack()
    with st:
        def sem(name):
            return st.enter_context(nc.semaphore(name))

        PE = Eng("pe", [sem(f"pe{i}") for i in range(4)])
        ACT = Eng("act", [sem(f"act{i}") for i in range(3)])
        DVE = Eng("dve", [sem(f"dve{i}") for i in range(3)])
        POOL = Eng("pool", [sem(f"pool{i}") for i in range(2)])
        SP = Eng("sp", [])
        ENGS = [PE, ACT, DVE, POOL, SP]
        dsems = [DSem(sem(f"d{i}")) for i in range(34)]
        sw_dsems = [DSem(sem(f"sw{i}")) for i in range(8)]
        ds_idx = [0]
        sw_idx = [0]
        allbufs = []

        phase_no = [0]
        stopped = [False]

        def newds(sw=False):
            if sw:
                d = sw_dsems[sw_idx[0]]
                sw_idx[0] += 1
                return d
            d = dsems[ds_idx[0]]
            ds_idx[0] += 1
            return d

        def issue(E, fn, R=(), W=(), ds=None):
            if stopped[0]:
                return
            need = {}

            def add(tok):
                if tok is None:
                    return
                s, val, key, ek, gen = tok
                if ek == E.key and ds is None:
                    if ek == "pe":
                        return
                    if gen == E.gen:
                        between = E.cnt - val
                    elif gen == E.gen - 1:
                        between = E.cnt + (GEN_MAX - val)
                    else:
                        between = 99
                    if between >= 3:
                        return
                if E.waited.get(key, 0) >= val:
                    return
                if key not in need or need[key][1] < val:
                    need[key] = (s, val)

            for b in R:
                add(b.w)
                if b.psum:
                    for tk in b.r.values():
                        if tk[3] != E.key:
                            add(tk)
            for b in W:
                add(b.w)
                for tk in b.r.values():
                    add(tk)
            waits = list(need.items())
            for key, (s, val) in waits:
                E.waited[key] = val
            if ds is None:
                if E.cnt >= GEN_MAX:
                    E.gen += 1
                    E.cnt = 0
                E.cnt += 1
                tok = (E.sem, E.cnt, E.sem.num, E.key, E.gen)
                isem, ival = E.sem, 1
            else:
                ds.cnt += 16
                tok = (ds.sem, ds.cnt, ds.sem.num, None, 0)
                isem, ival = ds.sem, 16

            def emit(e, waits=waits, fn=fn, isem=isem, ival=ival):
                for key, (s, val) in waits:
                    e.wait_ge(s, val)
                fn(e).then_inc(isem, ival)

            E.prog.append(emit)
            for b in W:
                b.w = tok
                b.r = {}
            for b in R:
                if b.w is tok:
                    continue
                old = b.r.get(tok[2])
                if old is None or old[1] < tok[1]:
                    b.r[tok[2]] = tok

        def flush_phase():
            if stopped[0]:
                return
            used = [d for d in dsems + sw_dsems if d.cnt > 0]

            def drain(e, used=[(d.sem, d.cnt) for d in used]):
                for s, v in used:
                    e.wait_ge(s, v)

            SP.prog.append(drain)
            with nc.Block() as blk:
                @blk.tensor
                def _(e):
                    for f in PE.prog:
                        f(e)

                @blk.scalar
                def _(e):
                    for f in ACT.prog:
                        f(e)

                @blk.vector
                def _(e):
                    for f in DVE.prog:
                        f(e)

                @blk.gpsimd
                def _(e):
                    for f in POOL.prog:
                        f(e)

                @blk.sync
                def _(e):
                    for f in SP.prog:
                        f(e)
            for E in ENGS:
                E.prog = []
            for b in allbufs:
                b.w = None
                b.r = {}
            ds_idx[0] = 0
            sw_idx[0] = 0
            phase_no[0] += 1
            if phase_no[0] >= nphase:
                stopped[0] = True

        def bl(xs):
            return [x.b if isinstance(x, TT) else x for x in xs]

        def mm(out, lhsT, rhs, start, stop, R, W):
            issue(PE, lambda e: e.matmul(out, lhsT=lhsT, rhs=rhs, start=start, stop=stop), bl(R), bl(W))

        def act(out, in_, func, R, W, bias=None, scale=None):
            kw = {}
            if bias is not None:
                kw["bias"] = bias
            if scale is not None:
                kw["scale"] = scale
            issue(ACT, lambda e: e.activation(out=out, in_=in_, func=func, **kw), bl(R), bl(W))

        def tt(E, out, in0, in1, op, R, W):
            issue(E, lambda e: e.tensor_tensor(out=out, in0=in0, in1=in1, op=op), bl(R), bl(W))

        def stt(E, out, in0, scalar, in1, op0, op1, R, W):
            issue(E, lambda e: e.scalar_tensor_tensor(out=out, in0=in0, scalar=scalar, in1=in1, op0=op0, op1=op1),
                  bl(R), bl(W))

        def ts(E, out, in0, s1, op0, R, W, s2=None, op1=None):
            if op1 is None:
                issue(E, lambda e: e.tensor_scalar(out=out, in0=in0, scalar1=s1, scalar2=None, op0=op0), bl(R), bl(W))
            else:
                issue(E, lambda e: e.tensor_scalar(out=out, in0=in0, scalar1=s1, scalar2=s2, op0=op0, op1=op1),
                      bl(R), bl(W))

        def cp(E, out, in_, R, W):
            if E is ACT:
                issue(E, lambda e: e.activation(out=out, in_=in_, func=AF.Copy), bl(R), bl(W))
            else:
                issue(E, lambda e: e.tensor_copy(out=out, in_=in_), bl(R), bl(W))

        def recip(out, in_, R, W):
            issue(DVE, lambda e: e.reciprocal(out=out, in_=in_), bl(R), bl(W))

        def mset(E, ap, val, W):
            issue(E, lambda e: e.memset(ap, val), [], bl(W))

        def dma(Q, out, in_, R, W, ds):
            issue(Q, lambda e: e.dma_start(out=out, in_=in_), bl(R), bl(W), ds=ds)

        uniq = [0]

        def sb(stack, name, shape, dt):
            uniq[0] += 1
            name = f"{name}_{uniq[0]}"
            t = stack.enter_context(nc.sbuf_tensor(name, list(shape), dt))
            x = TT(t, name)
            allbufs.append(x.b)
            return x

        PS = []
        PP = []
        for i in range(4):
            t = st.enter_context(nc.psum_tensor(f"psp{i}", [128, 1024], F32))
            PP.append(t)
            for hlf in range(2):
                x = TT(t[:, hlf * 512:(hlf + 1) * 512], f"psb{2 * i + hlf}")
                x.b.psum = True
                allbufs.append(x.b)
                PS.append(x)
        ps_rr = [0]

        class Pool_:
            def __init__(self, idxs):
                self.idxs = idxs
                self.i = 0

            def next(self):
                p = PS[self.idxs[self.i % len(self.idxs)]]
                self.i += 1
                return p

        vecs = sb(st, "vecs", [128, NV], F32)
        cst = sb(st, "cst", [128, 80], F32)
        ones_bf = sb(st, "ones_bf", [128, 128], BF16)
        ones_f = sb(st, "ones_f", [128, 128], F32)
        C_A1, C_B1, C_A1C, C_B1C, C_GT1, C_A2, C_B2, C_GT2, C_EPS, C_ONE = 0, 8, 16, 24, 32, 40, 48, 56, 64, 65
        eps_ap = cst.t[:, C_EPS:C_EPS + 1]

        late_pieces = []

        def build_phases():
            with ExitStack() as ph:
                vds = newds()
                dma(SP, vecs.t[:, :], vecs_d[:, :], [], [vecs], vds)
                mset(DVE, ones_bf.t[:, :], 1.0, [ones_bf])
                mset(DVE, ones_f.t[:, :], 1.0, [ones_f])
                mset(DVE, cst.t[:, C_EPS:C_EPS + 1], EPS, [cst])
                mset(DVE, cst.t[:, C_ONE:C_ONE + 1], 1.0, [cst])
                sT = sb(ph, "sT", [128, 16], F32)
                sT2 = sb(ph, "sT2", [128, 8, 2], F32)
                act(sT.t[:, :], vecs.t[:, V_C:V_C + 16], AF.Silu, [vecs], [sT])
                cp(DVE, sT2.t[:, :, 0], sT.t[:, 0:8], [sT], [sT2])
                cp(DVE, sT2.t[:, :, 1], sT.t[:, 8:16], [sT], [sT2])
                wst = [sb(ph, f"wada{i}", [128, 8, 512], F32) for i in range(2)]
                wds = [newds() for _ in range(2)]
                modp = PS[0]
                for pc in range(12):
                    w = wst[pc % 2]
                    dma(SP, w.t[:, :, :], w_ada_d[:, :, pc * 512:(pc + 1) * 512], [], [w], wds[pc % 2])
                    for cc in range(4):
                        col = pc * 4 + cc
                        for k in range(8):
                            mm(modp.t[:, col * 2:col * 2 + 2], w.t[:, k, cc * 128:(cc + 1) * 128], sT2.t[:, k, :],
                               k == 0, k == 7, [w, sT2], [modp])
                mod = sb(ph, "mod", [128, 48, 2], F32)
                cp(ACT, mod.t[:, :, :], modp.t[:, 0:96].rearrange("p (a b) -> p a b", b=2), [modp], [mod])
                modl = sb(ph, "modl", [128, 48], F32)
                modc = sb(ph, "modc", [128, 16], F32)
                tt(DVE, modl.t[:, :], mod.t[:, :, 0], vecs.t[:, V_BADA:V_BADA + 48], ALU.add, [mod, vecs], [modl])
                tt(DVE, modc.t[:, :], mod.t[:, 0:16, 1], vecs.t[:, V_BADA:V_BADA + 16], ALU.add, [mod, vecs], [modc])
                stt(DVE, cst.t[:, C_A1:C_A1 + 8], modl.t[:, 8:16], 1.0, vecs.t[:, V_G1:V_G1 + 8], ALU.add, ALU.mult,
                    [modl, vecs], [cst])
                cp(DVE, cst.t[:, C_B1:C_B1 + 8], modl.t[:, 0:8], [modl], [cst])
                stt(DVE, cst.t[:, C_A1C:C_A1C + 8], modc.t[:, 8:16], 1.0, vecs.t[:, V_G1:V_G1 + 8], ALU.add, ALU.mult,
                    [modc, vecs], [cst])
                cp(DVE, cst.t[:, C_B1C:C_B1C + 8], modc.t[:, 0:8], [modc], [cst])
                cp(DVE, cst.t[:, C_GT1:C_GT1 + 8], modl.t[:, 16:24], [modl], [cst])
                stt(DVE, cst.t[:, C_A2:C_A2 + 8], modl.t[:, 32:40], 1.0, vecs.t[:, V_G2:V_G2 + 8], ALU.add, ALU.mult,
                    [modl, vecs], [cst])
                cp(DVE, cst.t[:, C_B2:C_B2 + 8], modl.t[:, 24:32], [modl], [cst])
                cp(DVE, cst.t[:, C_GT2:C_GT2 + 8], modl.t[:, 40:48], [modl], [cst])

                stg = [sb(ph, f"stg{i}", [128, 4096], F32) for i in range(3)]
                stb = [sb(ph, f"stb{i}", [128, 4096], BF16) for i in range(3)]
                sds = [newds() for _ in range(3)]
                bds = [newds(True) for _ in range(3)]
                pieces = []

                def add_w(src, dst, nch, ncols, step, gcol=None, np_=128):
                    for c0 in range(0, ncols, step):
                        n = min(step, ncols - c0)
                        pieces.append((src[:, :, c0:c0 + n], dst[:, :, c0:c0 + n], nch, n, gcol, np_))

                add_w(w_in_d, win_s, 8, 4800, 480)
                add_w(wkr2_d, wkr_s, 8, 192, 192)
                add_w(w_uq_d, wuq_s, 3, 768, 768, V_QG)
                add_w(w_uqs_d, wuqs_s, 3, 768, 768, V_QG)
                add_w(w_ukv_d, wukv_s, 2, 1024, 1024, V_KVG)
                add_w(w_brm_d, wbrm_s, 4, D, D)
                add_w(w_brg_d, wbrg_s, 4, D, D)
                add_w(w_out_d, wout_s, 8, D, 512)
                add_w(w_up_d, wup_s, 8, 2 * DFF, 512)
                for j0 in range(0, NJ, 4):
                    nj = min(4, NJ - j0)
                    pieces.append((w_dn_d[:, j0:j0 + nj, :], wdn_s[:, j0:j0 + nj, :], nj, D, None, 128))
                cengs = [ACT, DVE, POOL]
                early = pieces[0:2] + pieces[10:15]
                late_pieces.extend(pieces[2:10] + pieces[15:])
                late_pieces.append((wdec_d[:, :], wdec_s[:, :], 1, 1024, None, 17))
                for i, (src, dst, nch, n, gcol, np_) in enumerate(early):
                    s_ = i % 3
                    sg, sbf = stg[s_], stb[s_]
                    sgv = sg.t[:, 0:nch * n].rearrange("p (c n) -> p c n", n=n)
                    sbv = sbf.t[:, 0:nch * n].rearrange("p (c n) -> p c n", n=n)
                    dma(SP, sgv, src, [], [sg], sds[s_])
                    if gcol is None:
                        cp(cengs[i % 3], sbf.t[:, 0:nch * n], sg.t[:, 0:nch * n], [sg], [sbf])
                    else:
                        for c in range(nch):
                            ts(DVE, sbv[:, c, :], sgv[:, c, :], vecs.t[:, gcol + c:gcol + c + 1], ALU.mult,
                               [sg, vecs], [sbf])
                    dma(POOL, dst, sbv, [sbf], [], bds[s_])
                flush_phase()

            def rms_block(ph_sq, xs, n, a_col, b_col, hs, pp, sd, rstd, tmp, divisor=1.0 / D):
                act(ph_sq.t[:, :, 0:n], xs.t[:, :, 0:n], AF.Square, [xs], [ph_sq])
                for n0 in range(0, n, 512):
                    nn = min(512, n - n0)
                    p = pp.next()
                    for k in range(8):
                        mm(p.t[:, 0:nn], ones_bf.t[:, :], ph_sq.t[:, k, n0:n0 + nn], k == 0, k == 7, [ones_bf, ph_sq], [p])
                    act(sd.t[:, n0:n0 + nn], p.t[:, 0:nn], AF.Ln, [p, cst], [sd], bias=eps_ap, scale=divisor)
                act(rstd.t[:, 0:n], sd.t[:, 0:n], AF.Exp, [sd], [rstd], scale=-0.5)
                for k in range(8):
                    tm = tmp[k % len(tmp)]
                    stt(DVE, tm.t[:, 0:n], xs.t[:, k, 0:n], cst.t[:, a_col + k:a_col + k + 1], rstd.t[:, 0:n],
                        ALU.mult, ALU.mult, [xs, cst, rstd], [tm])
                    act(hs.t[:, k, 0:n], tm.t[:, 0:n], AF.Identity, [tm, cst], [hs],
                        bias=cst.t[:, b_col + k:b_col + k + 1], scale=1.0)

            with ExitStack() as ph:
                xs = [sb(ph, f"xs{i}", [128, 8, 512], F32) for i in range(2)]
                xds = [newds() for _ in range(2)]
                hs = [sb(ph, f"hs{i}", [128, 8, 512], BF16) for i in range(2)]
                hds = [newds(True) for _ in range(2)]
                sq = sb(ph, "sq", [128, 8, 512], BF16)
                sd = sb(ph, "sd", [128, 512], F32)
                rstd = sb(ph, "rstd", [128, 512], F32)
                tmp = [sb(ph, f"tmp{i}", [128, 512], F32) for i in range(2)]
                pp = Pool_([0, 1])
                for b in range(NB):
                    t0, n = blk_range(b)
                    s_ = b % 2
                    dma(SP, xs[s_].t[:, :, 0:n], xT[:, :, t0:t0 + n], [], [xs[s_]], xds[s_])
                    rms_block(sq, xs[s_], n, C_A1 if b < 8 else C_A1C, C_B1 if b < 8 else C_B1C, hs[s_], pp, sd, rstd, tmp)
                    dma(POOL, hT_s[:, :, t0:t0 + n], hs[s_].t[:, :, 0:n], [hs[s_]], [], hds[s_])
                flush_phase()

            with ExitStack() as mla:
                Kt = sb(mla, "Kt", [96, 8, T], BF16)
                Va = sb(mla, "Va", [128, 34, 8, 65], BF16)
                qcn = sb(mla, "qcn", [128, 3, L], BF16)
                with ExitStack() as ph:
                    wqc = sb(ph, "wqc", [128, 8, 640], BF16)
                    wkr = sb(ph, "wkr", [128, 8, 192], BF16)
                    wukv = sb(ph, "wukv", [128, 2, 1024], BF16)
                    dma(SP, wqc.t[:, :, :], win_s[:, :, 0:640], [], [wqc], newds())
                    dma(SP, wkr.t[:, :, :], wkr_s[:, :, :], [], [wkr], newds())
                    dma(SP, wukv.t[:, :, :], wukv_s[:, :, :], [], [wukv], newds())
                    hs = [sb(ph, f"hs{i}", [128, 8, 512], BF16) for i in range(2)]
                    hds = [newds() for _ in range(2)]
                    csb = [sb(ph, f"csb{i}", [96, 2, 512], F32) for i in range(2)]
                    cds = [newds() for _ in range(2)]
                    lat_f = sb(ph, "lat_f", [128, 5, 512], F32)
                    lat_sq = sb(ph, "lat_sq", [128, 5, 512], BF16)
                    kvn = sb(ph, "kvn", [128, 2, 512], BF16)
                    sdq = sb(ph, "sdq", [128, 512], F32)
                    rsq = sb(ph, "rsq", [128, 512], F32)
                    sdk = sb(ph, "sdk", [128, 512], F32)
                    rsk = sb(ph, "rsk", [128, 512], F32)
                    t1 = sb(ph, "t1", [96, 512], F32)
                    t2 = sb(ph, "t2", [96, 512], F32)
                    kro = sb(ph, "kro", [96, 512], BF16)
                    mset(DVE, Va.t[:, :, :, 64:65], 1.0, [Va])
                    pp = Pool_([0, 1, 2, 3, 4, 5, 6, 7])
                    ev = [ACT, DVE]
                    evi = 0
                    for b in range(NB):
                        t0, n = blk_range(b)
                        s_ = b % 2
                        h = hs[s_]
                        dma(SP, h.t[:, :, 0:n], hT_s[:, :, t0:t0 + n], [], [h], hds[s_])
                        if b < 8:
                            dma(SP, csb[s_].t[64:96, :, :], cs_d[64:96, :, t0:t0 + n], [], [csb[s_]], cds[s_])
                        chunks = ([0, 1, 2] if b < 8 else []) + [3, 4]
                        for c in chunks:
                            p = pp.next()
                            for k in range(8):
                                mm(p.t[:, 0:n], wqc.t[:, k, c * 128:(c + 1) * 128], h.t[:, k, 0:n], k == 0, k == 7,
                                   [wqc, h], [p])
                            act(lat_sq.t[:, c, 0:n], p.t[:, 0:n], AF.Square, [p], [lat_sq])
                            cp(DVE, lat_f.t[:, c, 0:n], p.t[:, 0:n], [p], [lat_f])
                        if b < 8:
                            p = pp.next()
                            for c in range(3):
                                mm(p.t[:, 0:n], ones_bf.t[:, :], lat_sq.t[:, c, 0:n], c == 0, c == 2, [ones_bf, lat_sq], [p])
                            act(sdq.t[:, 0:n], p.t[:, 0:n], AF.Ln, [p, cst], [sdq], bias=eps_ap, scale=1.0 / 384)
                            act(rsq.t[:, 0:n], sdq.t[:, 0:n], AF.Exp, [sdq], [rsq], scale=-0.5)
                            for c in range(3):
                                tt(DVE, qcn.t[:, c, t0:t0 + n], lat_f.t[:, c, 0:n], rsq.t[:, 0:n], ALU.mult,
                                   [lat_f, rsq], [qcn])
                        p = pp.next()
                        for c in range(2):
                            mm(p.t[:, 0:n], ones_bf.t[:, :], lat_sq.t[:, 3 + c, 0:n], c == 0, c == 1, [ones_bf, lat_sq], [p])
                        act(sdk.t[:, 0:n], p.t[:, 0:n], AF.Ln, [p, cst], [sdk], bias=eps_ap, scale=1.0 / 256)
                        act(rsk.t[:, 0:n], sdk.t[:, 0:n], AF.Exp, [sdk], [rsk], scale=-0.5)
                        for c in range(2):
                            tt(DVE, kvn.t[:, c, 0:n], lat_f.t[:, 3 + c, 0:n], rsk.t[:, 0:n], ALU.mult, [lat_f, rsk], [kvn])
                        for hh in range(8):
                            p = pp.next()
                            for c in range(2):
                                mm(p.t[0:64, 0:n], wukv.t[:, c, hh * 64:(hh + 1) * 64], kvn.t[:, c, 0:n], c == 0, c == 1,
                                   [wukv, kvn], [p])
                            cp(ev[evi % 2], Kt.t[0:64, hh, t0:t0 + n], p.t[0:64, 0:n], [p], [Kt])
                            evi += 1
                        for tl in range(n // 128):
                            kt = (t0 + tl * 128) // 128
                            p = pp.next()
                            for c in range(2):
                                mm(p.t[:, 0:512], kvn.t[:, c, tl * 128:(tl + 1) * 128], wukv.t[:, c, 512:1024],
                                   c == 0, c == 1, [wukv, kvn], [p])
                            cp(ev[evi % 2], Va.t[:, kt, :, 0:64], p.t[:, 0:512].rearrange("p (h d) -> p h d", d=64), [p], [Va])
                            evi += 1
                        pk = pp.next()
                        for k in range(8):
                            mm(pk.t[0:96, 0:n], wkr.t[:, k, 0:96], h.t[:, k, 0:n], k == 0, k == 7, [wkr, h], [pk])
                        if b < 8:
                            pks = pp.next()
                            for k in range(8):
                                mm(pks.t[0:96, 0:n], wkr.t[:, k, 96:192], h.t[:, k, 0:n], k == 0, k == 7, [wkr, h], [pks])
                            tt(DVE, t1.t[64:96, 0:n], pk.t[64:96, 0:n], csb[s_].t[64:96, 0, 0:n], ALU.mult, [pk, csb[s_]], [t1])
                            tt(DVE, t2.t[64:96, 0:n], pks.t[64:96, 0:n], csb[s_].t[64:96, 1, 0:n], ALU.mult, [pks, csb[s_]], [t2])
                            tt(POOL, kro.t[64:96, 0:n], t1.t[64:96, 0:n], t2.t[64:96, 0:n], ALU.add, [t1, t2], [kro])
                        else:
                            cp(ACT, kro.t[64:96, 0:n], pk.t[64:96, 0:n], [pk], [kro])
                        for hh in range(8):
                            cp(POOL, Kt.t[64:96, hh, t0:t0 + n], kro.t[64:96, 0:n], [kro], [Kt])
                    flush_phase()

                with ExitStack() as ph:
                    wuq = sb(ph, "wuq", [128, 3, 768], BF16)
                    wuqs = sb(ph, "wuqs", [128, 3, 768], BF16)
                    dma(SP, wuq.t[:, :, :], wuq_s[:, :, :], [], [wuq], newds())
                    dma(SP, wuqs.t[:, :, :], wuqs_s[:, :, :], [], [wuqs], newds())
                    csb = [sb(ph, f"csb{i}", [96, 2, 512], F32) for i in range(2)]
                    cds = [newds() for _ in range(2)]
                    Qt = [sb(ph, f"Qt{i}", [96, 512], BF16) for i in range(2)]
                    Pt = [sb(ph, f"Pt{i}", [128, 1024], BF16) for i in range(4)]
                    t1 = sb(ph, "t1", [96, 512], F32)
                    t2 = sb(ph, "t2", [96, 512], F32)
                    osb = [sb(ph, f"osb{i}", [65, 512], F32) for i in range(2)]
                    bcs = sb(ph, "bcs", [64, 512], F32)
                    ost = [sb(ph, f"ost{i}", [64, 512], BF16) for i in range(2)]
                    ods = [newds(True) for _ in range(2)]
                    pO_ = PS[6]
                    pQB = PS[7]
                    prc = 0
                    stgC = [sb(ph, f"stgC{i}", [128, 4096], F32) for i in range(2)]
                    stbC = [sb(ph, f"stbC{i}", [128, 4096], BF16) for i in range(1)]
                    sdsC = [newds() for _ in range(2)]
                    bdsC = [newds(True) for _ in range(1)]
                    late = list(late_pieces)
                    cstate = {"pend": None, "k": 0}

                    def cast_step():
                        if cstate["pend"] is not None:
                            slot, (src, dst, nch, n, gcol, np_) = cstate["pend"]
                            sg, sbf = stgC[slot], stbC[0]
                            cp(DVE, sbf.t[0:np_, 0:nch * n], sg.t[0:np_, 0:nch * n], [sg], [sbf])
                            if np_ == 128:
                                sbv = sbf.t[:, 0:nch * n].rearrange("p (c n) -> p c n", n=n)
                            else:
                                sbv = sbf.t[0:np_, 0:n]
                            dma(POOL, dst, sbv, [sbf], [], bdsC[0])
                            cstate["pend"] = None
                        if late:
                            piece = late.pop(0)
                            src, dst, nch, n, gcol, np_ = piece
                            slot = cstate["k"] % 2
                            cstate["k"] += 1
                            sg = stgC[slot]
                            if np_ == 128:
                                sgv = sg.t[:, 0:nch * n].rearrange("p (c n) -> p c n", n=n)
                            else:
                                sgv = sg.t[0:np_, 0:n]
                            dma(SP, sgv, src, [], [sg], sdsC[slot])
                            cstate["pend"] = (slot, piece)

                    def qgen(it_):
                        qb_, hh_ = it_ // 8, it_ % 8
                        q0_ = qb_ * 512
                        cs_ = csb[qb_ % 2]
                        if hh_ == 0:
                            dma(SP, cs_.t[64:96, :, :], cs_d[64:96, :, q0_:q0_ + 512], [], [cs_], cds[qb_ % 2])
                        Q = Qt[it_ % 2]
                        for hf in range(2):
                            cl = slice(hf * 256, (hf + 1) * 256)
                            qc = slice(q0_ + hf * 256, q0_ + (hf + 1) * 256)
                            for c in range(3):
                                mm(pQB.t[0:96, 0:256], wuq.t[:, c, hh_ * 96:(hh_ + 1) * 96], qcn.t[:, c, qc],
                                   c == 0, c == 2, [wuq, qcn], [pQB])
                            for c in range(3):
                                mm(pQB.t[0:96, 256:512], wuqs.t[:, c, hh_ * 96:(hh_ + 1) * 96], qcn.t[:, c, qc],
                                   c == 0, c == 2, [wuqs, qcn], [pQB])
                            cp(DVE, Q.t[0:64, cl], pQB.t[0:64, 0:256], [pQB], [Q])
                            tt(DVE, t1.t[64:96, cl], pQB.t[64:96, 0:256], cs_.t[64:96, 0, cl], ALU.mult, [pQB, cs_], [t1])
                            tt(DVE, t2.t[64:96, cl], pQB.t[64:96, 256:512], cs_.t[64:96, 1, cl], ALU.mult, [pQB, cs_], [t2])
                        tt(POOL, Q.t[64:96, :], t1.t[64:96, :], t2.t[64:96, :], ALU.add, [t1, t2], [Q])

                    def norm_tail(it_):
                        qb_, hh_ = it_ // 8, it_ % 8
                        q0_ = qb_ * 512
                        ob = osb[it_ % 2]
                        mm(pQB.t[0:64, :], ones_f.t[64:65, 0:64], ob.t[64:65, :], True, True, [ones_f, ob], [pQB])
                        cp(DVE, bcs.t[:, :], pQB.t[0:64, :], [pQB], [bcs])
                        o_ = ost[it_ % 2]
                        tt(DVE, o_.t[:, :], ob.t[0:64, :], bcs.t[:, :], ALU.mult, [ob, bcs], [o_])
                        hp = (hh_ % 2) * 64
                        dma(POOL, om_s[hp:hp + 64, hh_ // 2, q0_:q0_ + 512], o_.t[:, :], [o_], [], ods[it_ % 2])

                    NKP = 17

                    def s_step(hh_, Q, kp):
                        nonlocal_prc = prcbox
                        pi = nonlocal_prc[0] % 3
                        P = Pt[nonlocal_prc[0] % 4]
                        nonlocal_prc[0] += 1
                        b0, b1 = PS[2 * pi], PS[2 * pi + 1]
                        for u, bk in ((0, b0), (1, b1)):
                            kt = 2 * kp + u
                            mm(bk.t[:, :], Kt.t[0:96, hh_, kt * 128:(kt + 1) * 128], Q.t[0:96, :], True, True,
                               [Kt, Q], [bk])
                        act(P.t[:, :], PP[pi][:, :], AF.Exp, [b0, b1], [P], scale=MLA_SCALE)
                        return P

                    prcbox = [0]
                    qgen(0)
                    for it in range(64):
                        qb, hh = it // 8, it % 8
                        Q = Qt[it % 2]
                        Ps = {}
                        Ps[0] = s_step(hh, Q, 0)
                        Ps[1] = s_step(hh, Q, 1)
                        for kp in range(NKP):
                            if kp + 2 < NKP:
                                Ps[kp + 2] = s_step(hh, Q, kp + 2)
                            Pp = Ps.pop(kp)
                            for u in range(2):
                                kt = 2 * kp + u
                                mm(pO_.t[0:65, :], Va.t[:, kt, hh, 0:65], Pp.t[:, u * 512:(u + 1) * 512],
                                   kt == 0, kt == 33, [Va, Pp], [pO_])
                            if kp == 3 and it >= 1:
                                norm_tail(it - 1)
                            if kp == 8 and it + 1 < 64:
                                qgen(it + 1)
                        ob = osb[it % 2]
                        cp(DVE, ob.t[0:65, :], pO_.t[0:65, :], [pO_], [ob])
                        recip(ob.t[64:65, :], ob.t[64:65, :], [ob], [ob])
                        cast_step()
                    norm_tail(63)
                    while late or cstate["pend"] is not None:
                        cast_step()
                    flush_phase()

            with ExitStack() as ph:
                wg = sb(ph, "wg", [128, 8, 2080], BF16)
                wdec = sb(ph, "wdec", [17, 1024], BF16)
                tri = sb(ph, "tri", [128, 4, 128], F32)
                wgB = {}
                for nm, (c0, c1) in (("l", (2048, 2080)), ("k", (512, 1024)), ("v", (1024, 1536)),
                                     ("q", (0, 512)), ("r", (1536, 2048))):
                    bb_ = Buf("wg_" + nm)
                    allbufs.append(bb_)
                    wgB[nm] = bb_
                    dma(SP, wg.t[:, :, c0:c1], win_s[:, :, O_GQ + c0:O_GQ + c1], [], [bb_], newds())

                def wgbuf(col):
                    return wgB["q" if col < 512 else "k" if col < 1024 else "v" if col < 1536 else "r" if col < 2048 else "l"]
                dma(SP, wdec.t[:, :], wdec_s[:, :], [], [wdec], newds())
                dma(SP, tri.t[:, :, :], tri_d[:, 0:4, :], [], [tri], newds())
                TRI_F, TRIS_F, TRI_B, TRIS_B = 0, 1, 2, 3
                Sf_in = sb(ph, "Sf_in", [128, 32, 512], BF16)
                Sst = [sb(ph, f"Sst{i}", [128, 512], F32) for i in range(2)]
                Sb_bf = sb(ph, "Sb_bf", [128, 512], BF16)
                hs = [sb(ph, f"hs{i}", [128, 8, 512], BF16) for i in range(2)]
                hds = [newds() for _ in range(2)]
                lfa = [[sb(ph, f"lfa{d}{i}", [32, 128], BF16) for i in range(2)] for d in range(2)]
                e_ = [[sb(ph, f"e{d}{i}", [128, 512], F32) for i in range(2)] for d in range(2)]
                Lt = [[sb(ph, f"L{d}{i}", [128, 512], F32) for i in range(2)] for d in range(2)]
                E3 = [sb(ph, f"E3{i}", [128, 512], F32) for i in range(2)]
                dec = [sb(ph, f"dec{i}", [128, 4], F32) for i in range(2)]
                kd = [sb(ph, f"kd{i}", [128, 512], BF16) for i in range(2)]
                vb = [sb(ph, f"vb{i}", [128, 512], BF16) for i in range(2)]
                Eq = [[sb(ph, f"Eq{d}{i}", [128, 512], F32) for i in range(2)] for d in range(2)]
                Ek = [[sb(ph, f"Ek{d}{i}", [128, 512], F32) for i in range(2)] for d in range(2)]
                qe = [[sb(ph, f"qe{d}{i}", [128, 512], BF16) for i in range(2)] for d in range(2)]
                ke = [[sb(ph, f"ke{d}{i}", [128, 512], BF16) for i in range(2)] for d in range(2)]
                Amf = sb(ph, "Amf", [128, 512], F32)
                Amb = sb(ph, "Amb", [128, 512], F32)
                Amt = [sb(ph, f"Amt{i}", [128, 512], BF16) for i in range(2)]
                gsq = sb(ph, "gsq", [128, 512], BF16)
                gsd = sb(ph, "gsd", [128, 512], F32)
                grs = sb(ph, "grs", [128, 512], F32)
                qTb = [sb(ph, f"qTb{i}", [128, 4, 512], F32) for i in range(2)]
                kTb = [sb(ph, f"kTb{i}", [128, 4, 512], F32) for i in range(2)]
                srb = [sb(ph, f"srb{i}", [128, 4, 512], F32) for i in range(2)]
                fblk = {"b": None, "slot": -1}
                fblk_of = {}
                y1 = sb(ph, "y1", [128, 512], F32)
                ys = [sb(ph, f"ys{i}", [128, 4, 128], BF16) for i in range(2)]
                yds = [newds(True) for _ in range(2)]
                for d in range(2):
                    for i in range(2):
                        mset(DVE, lfa[d][i].t[:, :], 1.0, [lfa[d][i]])
                mset(DVE, Sst[0].t[:, :], 0.0, [Sst[0]])
                mset(DVE, Sst[1].t[:, :], 0.0, [Sst[1]])
                mset(POOL, Sb_bf.t[:, :], 0.0, [Sb_bf])
                pp = Pool_([4, 5, 6, 7])
                pKV = Pool_([0, 1, 2, 3])
                cnt = [0]
                hload = [0]
                cur_blk = [None, None]

                def get_h(n_):
                    b = n_ // 4 if n_ < 32 else 8
                    if cur_blk[0] != b:
                        s_ = hload[0] % 2
                        hload[0] += 1
                        t0, n = blk_range(b)
                        dma(SP, hs[s_].t[:, :, 0:n], hT_s[:, :, t0:t0 + n], [], [hs[s_]], hds[s_])
                        cur_blk[0] = b
                        cur_blk[1] = hs[s_]
                    off = (n_ % 4) * 128 if n_ < 32 else (n_ - 32) * 128
                    return cur_blk[1], off

                def gate_proj(h, off, d, i):
                    p = pp.next()
                    for k in range(8):
                        mm(p.t[0:16, 0:128], wg.t[:, k, 2048 + d * 16:2064 + d * 16], h.t[:, k, off:off + 128],
                           k == 0, k == 7, [wgB["l"], h], [p])
                    la = lfa[d][i]
                    cp(ACT, la.t[0:16, :], p.t[0:16, 0:128], [p], [la])

                def gate_x(d, i):
                    la = lfa[d][i]
                    px = pp.next()
                    mm(px.t[:, :], la.t[0:17, :], wdec.t[0:17, d * 512:(d + 1) * 512], True, True, [la, wdec], [px])
                    act(e_[d][i].t[:, :], px.t[:, :], AF.Exp, [px], [e_[d][i]], scale=-1.0)
                    act(Lt[d][i].t[:, :], e_[d][i].t[:, :], AF.Ln, [e_[d][i], cst], [Lt[d][i]],
                        bias=cst.t[:, C_ONE:C_ONE + 1], scale=1.0)
                    return Lt[d][i]

                def proj_tok(h, off, col0):
                    p = pKV.next()
                    for k in range(8):
                        mm(p.t[:, :], h.t[:, k, off:off + 128], wg.t[:, k, col0:col0 + 512], k == 0, k == 7, [wgbuf(col0), h], [p])
                    return p

                def su_prep(d, i, Lx, pk, pv):
                    tris = TRIS_F if d == 0 else TRIS_B
                    pr = pp.next()
                    mm(pr.t[:, :], tri.t[:, tris, :], Lx.t[:, :], True, True, [tri, Lx], [pr])
                    act(E3[i].t[:, :], pr.t[:, :], AF.Exp, [pr], [E3[i]], scale=-1.0 / 16)
                    pt = pp.next()
                    for hh in range(4):
                        mm(pt.t[:, hh:hh + 1], Lx.t[:, hh * 128:(hh + 1) * 128], ones_f.t[:, 0:1], True, True,
                           [Lx, ones_f], [pt])
                    act(dec[i].t[:, :], pt.t[:, 0:4], AF.Exp, [pt], [dec[i]], scale=-1.0 / 16)
                    tt(DVE, kd[i].t[:, :], pk.t[:, :], E3[i].t[:, :], ALU.mult, [pk, E3[i]], [kd[i]])
                    cp(ACT, vb[i].t[:, :], pv.t[:, :], [pv], [vb[i]])

                def su_mm(i):
                    pu = pp.next()
                    for hh in range(4):
                        mm(pu.t[:, hh * 128:(hh + 1) * 128], kd[i].t[:, hh * 128:(hh + 1) * 128],
                           vb[i].t[:, hh * 128:(hh + 1) * 128], True, True, [kd[i], vb[i]], [pu])
                    return pu

                def apply_update(d, i, pu):
                    S = Sst[d]
                    for hh in range(4):
                        sl = slice(hh * 128, (hh + 1) * 128)
                        stt(DVE, S.t[:, sl], S.t[:, sl], dec[i].t[:, hh:hh + 1], pu.t[:, sl], ALU.mult, ALU.add,
                            [S, dec[i], pu], [S])

                def A1a(n_):
                    i = cnt[0] % 2
                    cnt[0] += 1
                    h, off = get_h(n_)
                    gate_proj(h, off, 0, i)
                    pk = proj_tok(h, off, 512)
                    pv = proj_tok(h, off, 1024)
                    Lx = gate_x(0, i)
                    return (n_, i, Lx, pk, pv)

                def A1b(c):
                    n_, i, Lx, pk, pv = c
                    su_prep(0, i, Lx, pk, pv)

                def B1(c):
                    n_, i, Lx, pk, pv = c
                    if n_ < 32:
                        cp(ACT, Sf_in.t[:, n_, :], Sst[0].t[:, :], [Sst[0]], [Sf_in])
                    pu = su_mm(i)
                    apply_update(0, i, pu)

                order1 = [32, 33] + list(range(32))
                prevc = None
                for n_ in order1:
                    c = A1a(n_)
                    if prevc is not None:
                        B1(prevc)
                    A1b(c)
                    prevc = c
                B1(prevc)

                cur_blk[0] = None
                v3 = lambda t_: t_.t[:, :].rearrange("p (h c) -> p h c", c=128)

                def A2a(n_):
                    i = cnt[0] % 2
                    cnt[0] += 1
                    h, off = get_h(n_)
                    gate_proj(h, off, 1, i)
                    if n_ < 32:
                        gate_proj(h, off, 0, i)
                    pk = proj_tok(h, off, 512)
                    pv = proj_tok(h, off, 1024)
                    Lb = gate_x(1, i)
                    Lf = gate_x(0, i) if n_ < 32 else None
                    if n_ < 32:
                        b_ = n_ // 4
                        if fblk["b"] != b_:
                            fblk["b"] = b_
                            fblk["slot"] += 1
                            fs = fblk["slot"] % 2
                            evs = [ACT, DVE]
                            ei = 0
                            for col0, dst, fn_ in ((0, qTb[fs], None), (512, kTb[fs], None), (1536, srb[fs], AF.Silu)):
                                for hh in range(4):
                                    p = pp.next()
                                    for k in range(8):
                                        mm(p.t[:, :], wg.t[:, k, col0 + hh * 128:col0 + (hh + 1) * 128], h.t[:, k, 0:512],
                                           k == 0, k == 7, [wgbuf(col0), h], [p])
                                    if fn_ is not None:
                                        act(dst.t[:, hh, :], p.t[:, :], fn_, [p], [dst])
                                    else:
                                        cp(evs[ei % 2], dst.t[:, hh, :], p.t[:, :], [p], [dst])
                                        ei += 1
                        fblk_of[n_] = fblk["slot"] % 2
                    return (n_, i, Lb, Lf, pk, pv, off)

                def A2b(c):
                    n_, i, Lb, Lf, pk, pv, off = c
                    if n_ < 32:
                        fs = fblk_of[n_]
                        for d, Lx, trid in ((0, Lf, TRI_F), (1, Lb, TRI_B)):
                            pbt = pp.next()
                            for hh in range(4):
                                mm(pbt.t[:, hh * 128:(hh + 1) * 128], Lx.t[:, hh * 128:(hh + 1) * 128], tri.t[:, trid, :],
                                   True, True, [Lx, tri], [pbt])
                            act(Eq[d][i].t[:, :], pbt.t[:, :], AF.Exp, [pbt], [Eq[d][i]], scale=-1.0 / 16)
                            act(Ek[d][i].t[:, :], pbt.t[:, :], AF.Exp, [pbt], [Ek[d][i]], scale=1.0 / 16)
                    su_prep(1, i, Lb, pk, pv)
                    if n_ < 32:
                        for d in range(2):
                            stt(DVE, v3(qe[d][i]), qTb[fs].t[:, :, off:off + 128], GLA_QSCALE, v3(Eq[d][i]), ALU.mult,
                                ALU.mult, [qTb[fs], Eq[d][i]], [qe[d][i]])
                        for d in range(2):
                            tt(DVE, v3(ke[d][i]), kTb[fs].t[:, :, off:off + 128], v3(Ek[d][i]), ALU.mult,
                               [kTb[fs], Ek[d][i]], [ke[d][i]])

                def B2(c):
                    n_, i, Lb, Lf, pk, pv, off = c
                    if n_ >= 32:
                        pu = su_mm(i)
                        apply_update(1, i, pu)
                        cp(ACT, Sb_bf.t[:, :], Sst[1].t[:, :], [Sst[1]], [Sb_bf])
                        return
                    pa = []
                    for d in range(2):
                        p = pp.next()
                        for hh in range(4):
                            sl = slice(hh * 128, (hh + 1) * 128)
                            mm(p.t[:, sl], ke[d][i].t[:, sl], qe[d][i].t[:, sl], True, True, [ke[d][i], qe[d][i]], [p])
                        pa.append(p)
                    for hh in range(4):
                        sl = slice(hh * 128, (hh + 1) * 128)
                        tt(DVE, Amf.t[:, sl], pa[0].t[:, sl], tri.t[:, TRI_F, :], ALU.mult, [pa[0], tri], [Amf])
                        tt(DVE, Amb.t[:, sl], pa[1].t[:, sl], tri.t[:, TRI_B, :], ALU.mult, [pa[1], tri], [Amb])
                    tt(DVE, Amt[i].t[:, :], Amf.t[:, :], Amb.t[:, :], ALU.add, [Amf, Amb], [Amt[i]])
                    pu = su_mm(i)
                    po = pp.next()
                    for hh in range(4):
                        sl = slice(hh * 128, (hh + 1) * 128)
                        mm(po.t[:, sl], Sf_in.t[:, n_, sl], qe[0][i].t[:, sl], True, False, [Sf_in, qe[0][i]], [po])
                        mm(po.t[:, sl], Sb_bf.t[:, sl], qe[1][i].t[:, sl], False, False, [Sb_bf, qe[1][i]], [po])
                        mm(po.t[:, sl], vb[i].t[:, sl], Amt[i].t[:, sl], False, True, [vb[i], Amt[i]], [po])
                    apply_update(1, i, pu)
                    cp(ACT, Sb_bf.t[:, :], Sst[1].t[:, :], [Sst[1]], [Sb_bf])
                    act(gsq.t[:, :], po.t[:, :], AF.Square, [po], [gsq])
                    pn = pp.next()
                    mm(pn.t[:, :], ones_bf.t[:, :], gsq.t[:, :], True, True, [ones_bf, gsq], [pn])
                    act(gsd.t[:, :], pn.t[:, :], AF.Ln, [pn, cst], [gsd], bias=eps_ap, scale=1.0 / 128)
                    act(grs.t[:, :], gsd.t[:, :], AF.Exp, [gsd], [grs], scale=-0.5)
                    stt(DVE, y1.t[:, :], po.t[:, :], vecs.t[:, V_GLAG:V_GLAG + 1], grs.t[:, :], ALU.mult, ALU.mult,
                        [po, vecs, grs], [y1])
                    y_ = ys[i]
                    fsb = fblk_of[n_]
                    tt(DVE, y_.t[:, :, :], v3(y1), srb[fsb].t[:, :, off:off + 128], ALU.mult, [y1, srb[fsb]], [y_])
                    dma(POOL, yT_s[:, :, n_ * 128:(n_ + 1) * 128], y_.t[:, :, :], [y_], [], yds[i])

                order2 = [33, 32] + list(range(31, -1, -1))
                prevc = None
                for n_ in order2:
                    c = A2a(n_)
                    if prevc is not None:
                        B2(prevc)
                    A2b(c)
                    prevc = c
                B2(prevc)
                flush_phase()

            with ExitStack() as ph:
                wgt = sb(ph, "wgt", [128, 8, 2048], BF16)
                wbrm = sb(ph, "wbrm", [128, 4, D], BF16)
                wbrg = sb(ph, "wbrg", [128, 4, D], BF16)
                wout = sb(ph, "wout", [128, 8, D], BF16)
                dma(SP, wgt.t[:, :, :], win_s[:, :, O_GATE:O_GATE + 2048], [], [wgt], newds())
                dma(SP, wbrm.t[:, :, :], wbrm_s[:, :, :], [], [wbrm], newds())
                dma(SP, wbrg.t[:, :, :], wbrg_s[:, :, :], [], [wbrg], newds())
                dma(SP, wout.t[:, :, :], wout_s[:, :, :], [], [wout], newds())
                hs = [sb(ph, f"hs{i}", [128, 8, 512], BF16) for i in range(2)]
                om = [sb(ph, f"om{i}", [128, 4, 512], BF16) for i in range(2)]
                yb = [sb(ph, f"yb{i}", [128, 4, 512], BF16) for i in range(2)]
                xs = [sb(ph, f"xs{i}", [128, 8, 512], F32) for i in range(2)]
                lds = [[newds() for _ in range(4)] for _ in range(2)]
                mT = sb(ph, "mT", [128, 8, 512], BF16)
                x1 = [sb(ph, f"x1{i}", [128, 8, 512], F32) for i in range(2)]
                xds = [newds(True) for _ in range(2)]
                sgm = [sb(ph, f"sgm{i}", [128, 512], F32) for i in range(2)]
                sgg = [sb(ph, f"sgg{i}", [128, 512], F32) for i in range(2)]
                u1 = [sb(ph, f"u1{i}", [128, 512], F32) for i in range(2)]
                u2 = [sb(ph, f"u2{i}", [128, 512], F32) for i in range(2)]
                pp = Pool_([0, 1, 2, 3, 4, 5, 6, 7])
                for b in range(8):
                    t0 = b * 512
                    s_ = b % 2
                    h, o_, y_, x_ = hs[s_], om[s_], yb[s_], xs[s_]
                    dma(SP, h.t[:, :, :], hT_s[:, :, t0:t0 + 512], [], [h], lds[s_][0])
                    dma(SP, o_.t[:, :, :], om_s[:, :, t0:t0 + 512], [], [o_], lds[s_][1])
                    dma(SP, y_.t[:, :, :], yT_s[:, :, t0:t0 + 512], [], [y_], lds[s_][2])
                    dma(SP, x_.t[:, :, :], xT[:, :, t0:t0 + 512], [], [x_], lds[s_][3])
                    for fc in range(8):
                        i = fc % 2
                        cs = slice(fc * 128, (fc + 1) * 128)
                        pgm = pp.next()
                        for k in range(8):
                            mm(pgm.t[:, :], wgt.t[:, k, fc * 128:(fc + 1) * 128], h.t[:, k, :], k == 0, k == 7, [wgt, h], [pgm])
                        pgg = pp.next()
                        for k in range(8):
                            mm(pgg.t[:, :], wgt.t[:, k, 1024 + fc * 128:1024 + (fc + 1) * 128], h.t[:, k, :], k == 0, k == 7,
                               [wgt, h], [pgg])
                        pbm = pp.next()
                        for k in range(4):
                            mm(pbm.t[:, :], wbrm.t[:, k, cs], o_.t[:, k, :], k == 0, k == 3, [wbrm, o_], [pbm])
                        pbg = pp.next()
                        for k in range(4):
                            mm(pbg.t[:, :], wbrg.t[:, k, cs], y_.t[:, k, :], k == 0, k == 3, [wbrg, y_], [pbg])
                        act(sgm[i].t[:, :], pgm.t[:, :], AF.Sigmoid, [pgm], [sgm[i]])
                        act(sgg[i].t[:, :], pgg.t[:, :], AF.Sigmoid, [pgg], [sgg[i]])
                        tt(DVE, u1[i].t[:, :], pbm.t[:, :], sgm[i].t[:, :], ALU.mult, [pbm, sgm[i]], [u1[i]])
                        tt(DVE, u2[i].t[:, :], pbg.t[:, :], sgg[i].t[:, :], ALU.mult, [pbg, sgg[i]], [u2[i]])
                        tt(POOL, mT.t[:, fc, :], u1[i].t[:, :], u2[i].t[:, :], ALU.add, [u1[i], u2[i]], [mT])
                    xo = x1[s_]
                    for fc in range(8):
                        pm = pp.next()
                        for k in range(8):
                            mm(pm.t[:, :], wout.t[:, k, fc * 128:(fc + 1) * 128], mT.t[:, k, :], k == 0, k == 7, [wout, mT], [pm])
                        stt(DVE, xo.t[:, fc, :], pm.t[:, :], cst.t[:, C_GT1 + fc:C_GT1 + fc + 1], x_.t[:, fc, :],
                            ALU.mult, ALU.add, [pm, cst, x_], [xo])
                    dma(POOL, x1_s[:, :, t0:t0 + 512], xo.t[:, :, :], [xo], [], xds[s_])
                flush_phase()

            with ExitStack() as ph:
                wup = sb(ph, "wup", [128, 8, 2 * DFF], BF16)
                wvb, wgb = [], []
                for q in range(4):
                    c0, c1 = q * 768, min((q + 1) * 768, DFF)
                    bv, bg = Buf(f"wupv{q}"), Buf(f"wupg{q}")
                    allbufs.extend([bv, bg])
                    wvb.append(bv)
                    wgb.append(bg)
                    dma(SP, wup.t[:, :, c0:c1], wup_s[:, :, c0:c1], [], [bv], newds())
                    dma(SP, wup.t[:, :, DFF + c0:DFF + c1], wup_s[:, :, DFF + c0:DFF + c1], [], [bg], newds())
                wdn = [sb(ph, f"wdn{i}", [128, NJ, 128], BF16) for i in range(2)]
                wnds = [newds() for _ in range(2)]
                x1e = [sb(ph, f"x1e{i}", [128, 8, 640], F32) for i in range(2)]
                xds = [newds() for _ in range(2)]
                h2s = [sb(ph, f"h2{i}", [128, 8, 640], BF16) for i in range(2)]
                sqk = [sb(ph, f"sqk{i}", [128, 640], BF16) for i in range(2)]
                sd = sb(ph, "sd", [128, 640], F32)
                tmpf = sb(ph, "tmpf", [128, 640], F32)
                aT = sb(ph, "aT", [128, NJ, 512], BF16)
                gsb = [sb(ph, f"gsb{i}", [128, 10, 66], BF16) for i in range(2)]
                identF = sb(ph, "identF", [128, 128], F32)
                dma(SP, identF.t[:, :], tri_d[:, 4, :], [], [identF], newds())
                dg = [sb(ph, f"dg{i}", [128, 9, 128], BF16) for i in range(2)]
                gl = [sb(ph, f"gl{i}", [128, 512], F32) for i in range(2)]
                ods = [newds(True) for _ in range(2)]
                for g in gsb:
                    mset(POOL, g.t[:, :, :], 0.0, [g])
                pG = Pool_([0, 1, 2, 3])
                pV = Pool_([4, 5])
                pC = Pool_([6])
                pX = Pool_([7])
                it = 0
                wdn_it = 0

                def geom(b):
                    t0 = b * 512
                    lo = max(t0 - 64, 0)
                    hi = min(t0 + 576, L)
                    return t0, lo, hi, hi - lo, t0 - lo

                def load_f(b):
                    t0, lo, hi, ne, off = geom(b)
                    dma(SP, x1e[b % 2].t[:, :, 0:ne], x1_s[:, :, lo:hi], [], [x1e[b % 2]], xds[b % 2])

                def sumsq_rstd(src_fn, n):
                    p0, p1 = PS[7], PS[6]
                    n0 = min(n, 512)
                    for k in range(8):
                        q = sqk[k % 2]
                        act(q.t[:, 0:n], src_fn(k), AF.Square, src_fn.bufs, [q])
                        mm(p0.t[:, 0:n0], ones_bf.t[:, :], q.t[:, 0:n0], k == 0, k == 7, [ones_bf, q], [p0])
                        if n > 512:
                            mm(p1.t[:, 0:n - 512], ones_bf.t[:, :], q.t[:, 512:n], k == 0, k == 7, [ones_bf, q], [p1])
                    act(sd.t[:, 0:n0], p0.t[:, 0:n0], AF.Ln, [p0, cst], [sd], bias=eps_ap, scale=1.0 / D)
                    if n > 512:
                        act(sd.t[:, 512:n], p1.t[:, 0:n - 512], AF.Ln, [p1, cst], [sd], bias=eps_ap, scale=1.0 / D)
                    act(sd.t[:, 0:n], sd.t[:, 0:n], AF.Exp, [sd], [sd], scale=-0.5)

                def rms_f(b):
                    t0, lo, hi, ne, off = geom(b)
                    xs_, hs_ = x1e[b % 2], h2s[b % 2]
                    fn = lambda k: xs_.t[:, k, 0:ne]
                    fn.bufs = [xs_]
                    sumsq_rstd(fn, ne)
                    for k in range(8):
                        stt(DVE, tmpf.t[:, 0:ne], xs_.t[:, k, 0:ne], cst.t[:, C_A2 + k:C_A2 + k + 1], sd.t[:, 0:ne],
                            ALU.mult, ALU.mult, [xs_, cst, sd], [tmpf])
                        act(hs_.t[:, k, 0:ne], tmpf.t[:, 0:ne], AF.Identity, [tmpf, cst], [hs_],
                            bias=cst.t[:, C_B2 + k:C_B2 + k + 1], scale=1.0)

                def conv_tail(jj, gg, pvv, dgg):
                    pc = pC.next()
                    for tap in range(9):
                        ky, kx = tap // 3, tap % 3
                        mm(pc.t[:, 0:512].rearrange("p (r c) -> p r c", c=64), dgg.t[:, tap, :],
                           gg.t[:, ky:ky + 8, kx:kx + 64], tap == 0, tap == 8, [dgg, gg], [pc])
                    glt = gl[jj % 2]
                    act(glt.t[:, :], pc.t[:, 0:512], AF.Gelu, [pc, vecs], [glt],
                        bias=vecs.t[:, V_CB + jj:V_CB + jj + 1], scale=1.0)
                    tt(DVE, aT.t[:, jj, :], glt.t[:, :], pvv.t[:, :], ALU.mult, [glt, pvv], [aT])

                load_f(0)
                rms_f(0)
                for b in range(8):
                    t0, lo, hi, ne, off = geom(b)
                    r_first = 1 if b == 0 else 0
                    nrows = ne // 64
                    xs_, h2 = x1e[b % 2], h2s[b % 2]
                    if b + 1 < 8:
                        load_f(b + 1)
                    if b == 7:
                        for g in gsb:
                            mset(POOL, g.t[:, 9, :], 0.0, [g])
                    prev = None
                    for j in range(NJ):
                        g = gsb[it % 2]
                        dgs = dg[it % 2]
                        for tap in range(9):
                            ts(DVE, dgs.t[:, tap, :], identF.t[:, :], vecs.t[:, V_CW + j * 9 + tap:V_CW + j * 9 + tap + 1],
                               ALU.mult, [identF, vecs], [dgs])
                        pg0 = pG.next()
                        pg1 = pG.next()
                        gc = slice(DFF + j * 128, DFF + (j + 1) * 128)
                        for k in range(8):
                            mm(pg0.t[:, 0:512], wup.t[:, k, gc], h2.t[:, k, 0:512], k == 0, k == 7, [wgb[j // 6], h2], [pg0])
                        n1 = ne - 512
                        for k in range(8):
                            mm(pg1.t[:, 0:n1], wup.t[:, k, gc], h2.t[:, k, 512:ne], k == 0, k == 7, [wgb[j // 6], h2], [pg1])
                        pv = pV.next()
                        for k in range(8):
                            mm(pv.t[:, :], wup.t[:, k, j * 128:(j + 1) * 128], h2.t[:, k, off:off + 512], k == 0, k == 7,
                               [wvb[j // 6], h2], [pv])
                        cp(ACT, g.t[:, r_first:r_first + 8, 1:65], pg0.t[:, 0:512].rearrange("p (r c) -> p r c", c=64), [pg0], [g])
                        cp(ACT, g.t[:, r_first + 8:r_first + nrows, 1:65],
                           pg1.t[:, 0:n1].rearrange("p (r c) -> p r c", c=64), [pg1], [g])
                        if prev is not None:
                            conv_tail(*prev)
                        prev = (j, g, pv, dgs)
                        it += 1
                    conv_tail(*prev)
                    if b + 1 < 8:
                        rms_f(b + 1)
                    for fc in range(8):
                        wn = wdn[wdn_it % 2]
                        dma(SP, wn.t[:, :, :], wdn_s[:, :, fc * 128:(fc + 1) * 128], [], [wn], wnds[wdn_it % 2])
                        wdn_it += 1
                        pd = pX.next()
                        for j in range(NJ):
                            mm(pd.t[:, :], wn.t[:, j, :], aT.t[:, j, :], j == 0, j == NJ - 1, [wn, aT], [pd])
                        stt(DVE, xs_.t[:, fc, off:off + 512], pd.t[:, :], cst.t[:, C_GT2 + fc:C_GT2 + fc + 1],
                            xs_.t[:, fc, off:off + 512], ALU.mult, ALU.add, [pd, cst, xs_], [xs_])
                    fn = lambda k, xs_=xs_, off=off: xs_.t[:, k, off:off + 512]
                    fn.bufs = [xs_]
                    sumsq_rstd(fn, 512)
                    for fc in range(8):
                        stt(DVE, xs_.t[:, fc, off:off + 512], xs_.t[:, fc, off:off + 512], vecs.t[:, V_FG + fc:V_FG + fc + 1],
                            sd.t[:, 0:512], ALU.mult, ALU.mult, [xs_, vecs, sd], [xs_])
                    dma(POOL, outT[:, :, t0:t0 + 512], xs_.t[:, :, off:off + 512], [xs_], [], ods[b % 2])
                flush_phase()

        try:
            build_phases()
        except StopBuild:
            pass
    return nc


_NC_CACHE = {}


def _perm32():
    return np.array(list(range(8, 16)) + list(range(0, 8)) + list(range(24, 32)) + list(range(16, 24)))


def prep_shared(inp):
    f = np.float32
    w_in = np.ascontiguousarray(inp["w_in"][0], dtype=f)
    kr = w_in[:, O_KR:O_KR + 32]
    z64 = np.zeros((D, 64), f)
    wkr2 = np.ascontiguousarray(np.concatenate([z64, kr, z64, kr[:, _perm32()]], axis=1))
    w_uq = np.ascontiguousarray(inp["w_uq"][0], dtype=f)
    idx = np.arange(768).reshape(8, 96).copy()
    idx[:, 64:] = idx[:, 64:][:, _perm32()]
    w_uqs = np.ascontiguousarray(w_uq[:, idx.reshape(-1)])
    w_ukv = inp["w_ukv"][0].reshape(256, 8, 128)
    w_ukv_p = np.ascontiguousarray(
        np.concatenate([w_ukv[:, :, :64].reshape(256, 512), w_ukv[:, :, 64:].reshape(256, 512)], axis=1), dtype=f)
    wdec = np.ascontiguousarray(np.concatenate(
        [np.concatenate([inp["gla_w_decay"][0, d], inp["gla_b_decay"][0, d][None, :]], axis=0) for d in range(2)],
        axis=1), dtype=f)
    s = np.arange(128)[:, None]
    c = np.arange(128)[None, :]
    tri = np.stack([(s <= c), (s > c), (s >= c), (s < c), (s == c)], axis=1).astype(f)
    t = np.arange(L)
    row = (t // 64).astype(f)
    col = (t % 64).astype(f)
    inv = (10000.0 ** (-np.arange(0, 16, 2, dtype=f) / 16)).astype(f)
    ar = row[None, :] * inv[:, None]
    ac = col[None, :] * inv[:, None]
    cosf = np.concatenate([np.cos(ar), np.cos(ar), np.cos(ac), np.cos(ac)], axis=0)
    sinf = np.concatenate([-np.sin(ar), np.sin(ar), -np.sin(ac), np.sin(ac)], axis=0)
    cs = np.zeros((96, 2, L), f)
    cs[64:, 0] = cosf
    cs[64:, 1] = sinf

    def pv(v, n):
        return np.asarray(v, dtype=f).reshape(n, 128).T

    shared_vec = np.zeros((128, NV), f)
    shared_vec[:, V_BADA:V_BADA + 48] = pv(inp["b_ada"][0], 48)
    shared_vec[:, V_G1:V_G1 + 8] = pv(inp["norm1_g"][0], 8)
    shared_vec[:, V_G2:V_G2 + 8] = pv(inp["norm2_g"][0], 8)
    shared_vec[:, V_FG:V_FG + 8] = pv(inp["final_g"], 8)
    shared_vec[:, V_QG:V_QG + 3] = pv(inp["q_norm_g"][0], 3)
    shared_vec[:, V_KVG:V_KVG + 2] = pv(inp["kv_norm_g"][0], 2)
    shared_vec[:, V_GLAG] = inp["gla_norm_g"][0]
    shared_vec[:, V_CB:V_CB + NJ] = pv(inp["conv_b"][0], NJ)
    shared_vec[:, V_CW:V_CW + NJ * 9] = inp["conv_w"][0].reshape(9, NJ, 128).transpose(2, 1, 0).reshape(128, NJ * 9)
    shared_vec[:, V_CC:V_CC + 8] = pv(inp["c_ctx"], 8)
    sh = {
        "w_ada": np.ascontiguousarray(inp["w_ada"][0], dtype=f),
        "w_in": w_in, "wkr2": wkr2, "w_uq": w_uq, "w_uqs": w_uqs, "w_ukv": w_ukv_p, "wdec": wdec,
        "w_brm": np.ascontiguousarray(inp["w_br_mla"][0], dtype=f),
        "w_brg": np.ascontiguousarray(inp["w_br_gla"][0], dtype=f),
        "w_out": np.ascontiguousarray(inp["w_out"][0], dtype=f),
        "w_up": np.ascontiguousarray(inp["w_up"][0], dtype=f),
        "w_dn": np.ascontiguousarray(inp["w_down"][0], dtype=f),
        "tri": tri, "cs": cs,
    }
    return sh, shared_vec


def make_in_maps(inp):
    sh, shared_vec = prep_shared(inp)
    maps = []
    for r in range(8):
        v = shared_vec.copy()
        v[:, V_C:V_C + 8] = np.asarray(inp["c"][r], np.float32).reshape(8, 128).T
        xTr = np.ascontiguousarray(
            np.concatenate([np.asarray(inp["x"][r]).T, np.asarray(inp["ctx"][r]).T], axis=1), dtype=np.float32)
        m = dict(sh)
        m["xT"] = xTr
        m["vecs"] = v
        maps.append(m)
    return maps


def kernel(**inputs):
    inp = {k: np.asarray(v) for k, v in inputs.items()}
    if "nc" not in _NC_CACHE:
        _NC_CACHE["nc"] = build(False)
    nc = _NC_CACHE["nc"]
    maps = make_in_maps(inp)
    res = run_bass_kernel_spmd(nc, maps, core_ids=list(range(8)))
    out = np.stack([np.ascontiguousarray(res.results[r]["outT"].T) for r in range(8)], axis=0)
    return out.astype(np.float32)
```
